# Optimizing a Trainium2 kernel written in Bass

```python
import jax, jax.numpy as jnp
from jax import lax
import numpy as np

D_MODEL = 2048
BATCH = 16
SEQ = 2048
DEPTH = 1

GRID_W = 64
CTX_LEN = 256
NA_HEADS = 8
NA_HEAD_DIM = 128
NA_WIN_H_MAX = 8
NA_WIN_W = 16
RET_HEADS = 8
RET_QK_DIM = 64
RET_V_DIM = 128
RET_CHUNK = 128
NA_WIDTH = NA_HEADS * NA_HEAD_DIM
RET_QK_WIDTH = RET_HEADS * RET_QK_DIM
RET_V_WIDTH = RET_HEADS * RET_V_DIM
MIX_WIDTH = NA_WIDTH + RET_V_WIDTH
IN_WIDTH = 3 * NA_WIDTH + 2 * RET_QK_WIDTH + 2 * RET_V_WIDTH
D_FF = -(-8 * D_MODEL // (3 * 256)) * 256
ROPE_BASE = 10000.0
EPS = 1e-6

kernel_name = 'hybrid_na_retention_dit_block'


def rms_norm(x, g):
    xf = x.astype(jnp.float32)
    y = xf * lax.rsqrt(jnp.mean(xf * xf, axis=-1, keepdims=True) + EPS)
    return (y * g.astype(jnp.float32)).astype(x.dtype)


def adaln(cvec, ada_w, ada_b):
    m = jax.nn.silu(cvec) @ ada_w + ada_b
    return jnp.split(m, 6, axis=-1)


def modulate(x, g, shift, scale):
    return rms_norm(x, g) * (1 + scale) + shift


def split_projection(p):
    B, L, _ = p.shape
    sizes = [NA_WIDTH, NA_WIDTH, NA_WIDTH, RET_QK_WIDTH, RET_QK_WIDTH, RET_V_WIDTH]
    offs = np.cumsum(sizes).tolist()
    na_q, na_k, na_v, r_q, r_k, r_v, r_g = jnp.split(p, offs, axis=-1)
    heads = lambda t, h: t.reshape(B, L, h, -1).transpose(0, 2, 1, 3)
    return (heads(na_q, NA_HEADS), heads(na_k, NA_HEADS), heads(na_v, NA_HEADS),
            heads(r_q, RET_HEADS), heads(r_k, RET_HEADS), heads(r_v, RET_HEADS), r_g)


def axial_rope(x, pos_row, pos_col):
    d = x.shape[-1]
    half = d // 2
    quarter = half // 2
    inv_freq = ROPE_BASE ** (-jnp.arange(quarter, dtype=jnp.float32) / quarter)

    def rot(xa, pos):
        ang = pos.astype(jnp.float32)[:, None] * inv_freq
        cos, sin = jnp.cos(ang), jnp.sin(ang)
        x1 = xa[..., :quarter].astype(jnp.float32)
        x2 = xa[..., quarter:].astype(jnp.float32)
        return jnp.concatenate([x1 * cos - x2 * sin, x1 * sin + x2 * cos], axis=-1)

    out = jnp.concatenate([rot(x[..., :half], pos_row), rot(x[..., half:], pos_col)], axis=-1)
    return out.astype(x.dtype)


def neighbourhood_attention(q, k, v, k_ctx, v_ctx, rpb):
    B, H, rows, W, d = q.shape
    win_h = min(NA_WIN_H_MAX, rows)
    n_loc = win_h * NA_WIN_W
    scale = d ** -0.5
    cols = jnp.arange(W)
    col_start = jnp.clip(cols - NA_WIN_W // 2, 0, W - NA_WIN_W)
    col_idx = col_start[:, None] + jnp.arange(NA_WIN_W)[None, :]
    dc = col_idx - cols[:, None]

    def row_block(r):
        rs = jnp.clip(r - win_h // 2, 0, rows - win_h)
        q_r = lax.dynamic_index_in_dim(q, r, axis=2, keepdims=False)
        k_band = lax.dynamic_slice_in_dim(k, rs, win_h, axis=2)
        v_band = lax.dynamic_slice_in_dim(v, rs, win_h, axis=2)
        k_win = k_band[:, :, :, col_idx]
        v_win = v_band[:, :, :, col_idx]
        dr = rs + jnp.arange(win_h) - r
        bias = rpb[:, dr[None, :, None] + NA_WIN_H_MAX - 1, dc[:, None, :] + NA_WIN_W - 1]
        s_loc = jnp.einsum('bhqd,bhiqjd->bhqij', q_r, k_win).astype(jnp.float32) * scale
        s_loc = (s_loc + bias.astype(jnp.float32)[None]).reshape(B, H, W, n_loc)
        s_ctx = jnp.einsum('bhqd,bhkd->bhqk', q_r, k_ctx).astype(jnp.float32) * scale
        p = jax.nn.softmax(jnp.concatenate([s_loc, s_ctx], axis=-1), axis=-1).astype(v.dtype)
        p_loc = p[..., :n_loc].reshape(B, H, W, win_h, NA_WIN_W)
        p_ctx = p[..., n_loc:]
        return (jnp.einsum('bhqij,bhiqjd->bhqd', p_loc, v_win)
                + jnp.einsum('bhqk,bhkd->bhqd', p_ctx, v_ctx))

    out = lax.map(row_block, jnp.arange(rows))
    return jnp.moveaxis(out, 0, 2).reshape(B, H, rows * W, d)


def context_attention(q, k, v):
    s = jnp.einsum('bhqd,bhkd->bhqk', q, k).astype(jnp.float32) * q.shape[-1] ** -0.5
    p = jax.nn.softmax(s, axis=-1).astype(v.dtype)
    return jnp.einsum('bhqk,bhkd->bhqd', p, v)


def retention_chunkwise(q, k, v, log_gamma, s0):
    B, H, L, dk = q.shape
    dv = v.shape[-1]
    C = RET_CHUNK
    n = L // C
    lg = log_gamma.astype(jnp.float32)
    idx = jnp.arange(C, dtype=jnp.float32)
    diff = idx[:, None] - idx[None, :]
    decay_in = jnp.where(diff >= 0, jnp.exp(lg[:, None, None] * jnp.maximum(diff, 0.0)), 0.0)
    decay_q = jnp.exp(lg[:, None] * (idx + 1.0))[..., None]
    decay_k = jnp.exp(lg[:, None] * (C - 1.0 - idx))[..., None]
    decay_chunk = jnp.exp(lg * C)[:, None, None]

    def chunks(t):
        return jnp.moveaxis(t.astype(jnp.float32).reshape(B, H, n, C, t.shape[-1]), 2, 0)

    qs, ks, vs = chunks(q * dk ** -0.5), chunks(k), chunks(v)

    def step(s, inp):
        qc, kc, vc = inp
        inner = jnp.einsum('bhid,bhjd->bhij', qc, kc) * decay_in
        o = (jnp.einsum('bhij,bhjv->bhiv', inner, vc)
             + jnp.einsum('bhid,bhdv->bhiv', qc * decay_q, s))
        s_new = s * decay_chunk + jnp.einsum('bhjd,bhjv->bhdv', kc * decay_k, vc)
        return s_new, o

    s_fin, o = lax.scan(step, s0, (qs, ks, vs))
    o = jnp.moveaxis(o, 0, 2).reshape(B, H, L, dv).astype(v.dtype)
    return o, s_fin


def merge_mixers(o_na, o_ret, r_g, w_out):
    B, _, L, _ = o_na.shape
    of = o_ret.astype(jnp.float32)
    mu = jnp.mean(of, axis=-1, keepdims=True)
    var = jnp.mean(jnp.square(of - mu), axis=-1, keepdims=True)
    o_ret_n = ((of - mu) * lax.rsqrt(var + EPS)).astype(o_ret.dtype)
    na = o_na.transpose(0, 2, 1, 3).reshape(B, L, NA_WIDTH)
    ret = o_ret_n.transpose(0, 2, 1, 3).reshape(B, L, RET_V_WIDTH) * jax.nn.silu(r_g)
    return jnp.concatenate([na, ret], axis=-1) @ w_out


def swiglu(h, w_gate, w_up, w_down):
    return (jax.nn.silu(h @ w_gate) * (h @ w_up)) @ w_down


def setup_inputs(seed: int = 0) -> dict:
    key = jax.random.key(seed)
    ks = jax.random.split(key, 18)
    f32 = jnp.float32

    def nrm(k, shape, scale):
        return jax.random.normal(k, shape, f32) * scale

    base_lg = jnp.log1p(-jnp.exp2(-5.0 - jnp.arange(RET_HEADS, dtype=f32)))
    return {
        'x': nrm(ks[0], (BATCH, SEQ, D_MODEL), 1.0),
        'c': nrm(ks[1], (BATCH, D_MODEL), 1.0),
        'ctx': nrm(ks[2], (BATCH, CTX_LEN, D_MODEL), 1.0),
        'c_ctx': nrm(ks[3], (D_MODEL,), 1.0),
        'ada_w': nrm(ks[4], (DEPTH, D_MODEL, 6 * D_MODEL), 0.5 * D_MODEL ** -0.5),
        'ada_b': nrm(ks[5], (DEPTH, 6 * D_MODEL), 0.02),
        'norm_pre_mix': 1.0 + nrm(ks[6], (DEPTH, D_MODEL), 0.05),
        'norm_post_mix': 1.0 + nrm(ks[7], (DEPTH, D_MODEL), 0.05),
        'norm_pre_ffn': 1.0 + nrm(ks[8], (DEPTH, D_MODEL), 0.05),
        'norm_post_ffn': 1.0 + nrm(ks[9], (DEPTH, D_MODEL), 0.05),
        'w_in': nrm(ks[10], (DEPTH, D_MODEL, IN_WIDTH), D_MODEL ** -0.5),
        'na_rpb': nrm(ks[11], (DEPTH, NA_HEADS, 2 * NA_WIN_H_MAX - 1, 2 * NA_WIN_W - 1), 0.1),
        'ret_log_gamma_fwd': base_lg * (1.0 + nrm(ks[12], (DEPTH, RET_HEADS), 0.05)),
        'ret_log_gamma_bwd': base_lg * (1.0 + nrm(ks[13], (DEPTH, RET_HEADS), 0.05)),
        'w_out': nrm(ks[14], (DEPTH, MIX_WIDTH, D_MODEL), MIX_WIDTH ** -0.5),
        'w_gate': nrm(ks[15], (DEPTH, D_MODEL, D_FF), D_MODEL ** -0.5),
        'w_up': nrm(ks[16], (DEPTH, D_MODEL, D_FF), D_MODEL ** -0.5),
        'w_down': nrm(ks[17], (DEPTH, D_FF, D_MODEL), D_FF ** -0.5),
    }


def reference(x, c, ctx, c_ctx, ada_w, ada_b, norm_pre_mix, norm_post_mix, norm_pre_ffn,
              norm_post_ffn, w_in, na_rpb, ret_log_gamma_fwd, ret_log_gamma_bwd, w_out,
              w_gate, w_up, w_down):
    B, L, _ = x.shape
    rows = L // GRID_W
    t = jnp.arange(L)
    pos_row, pos_col = t // GRID_W, t % GRID_W
    flip = lambda a: jnp.flip(a, axis=2)
    grid = lambda a: a.reshape(B, NA_HEADS, rows, GRID_W, NA_HEAD_DIM)
    zero_state = jnp.zeros((B, RET_HEADS, RET_QK_DIM, RET_V_DIM), jnp.float32)

    for li in range(DEPTH):
        sh1, sc1, g1, sh2, sc2, g2 = [m[:, None, :] for m in adaln(c, ada_w[li], ada_b[li])]
        csh1, csc1, cg1, csh2, csc2, cg2 = adaln(c_ctx, ada_w[li], ada_b[li])

        h = modulate(x, norm_pre_mix[li], sh1, sc1)
        hc = modulate(ctx, norm_pre_mix[li], csh1, csc1)
        na_q, na_k, na_v, r_q, r_k, r_v, r_g = split_projection(h @ w_in[li])
        cna_q, cna_k, cna_v, cr_q, cr_k, cr_v, cr_g = split_projection(hc @ w_in[li])

        o_na = neighbourhood_attention(grid(na_q), grid(na_k), grid(na_v), cna_k, cna_v, na_rpb[li])

        o_cf, s_cf = retention_chunkwise(cr_q, cr_k, cr_v, ret_log_gamma_fwd[li], zero_state)
        o_cb, s_cb = retention_chunkwise(flip(cr_q), flip(cr_k), flip(cr_v), ret_log_gamma_bwd[li], zero_state)
        rq = axial_rope(r_q, pos_row, pos_col)
        rk = axial_rope(r_k, pos_row, pos_col)
        o_f, _ = retention_chunkwise(rq, rk, r_v, ret_log_gamma_fwd[li], s_cf)
        o_b, _ = retention_chunkwise(flip(rq), flip(rk), flip(r_v), ret_log_gamma_bwd[li], s_cb)
        o_ret = o_f + flip(o_b)

        mix = merge_mixers(o_na, o_ret, r_g, w_out[li])
        x_new = x + g1 * rms_norm(mix, norm_post_mix[li])
        h2 = modulate(x_new, norm_pre_ffn[li], sh2, sc2)
        x_new = x_new + g2 * rms_norm(swiglu(h2, w_gate[li], w_up[li], w_down[li]), norm_post_ffn[li])

        if li + 1 < DEPTH:
            o_na_c = context_attention(cna_q, cna_k, cna_v)
            o_ret_c = o_cf + flip(o_cb)
            mix_c = merge_mixers(o_na_c, o_ret_c, cr_g, w_out[li])
            ctx = ctx + cg1 * rms_norm(mix_c, norm_post_mix[li])
            hc2 = modulate(ctx, norm_pre_ffn[li], csh2, csc2)
            ctx = ctx + cg2 * rms_norm(swiglu(hc2, w_gate[li], w_up[li], w_down[li]), norm_post_ffn[li])
        x = x_new
    return x
```

```python
import numpy as np
from contextlib import ExitStack
import concourse.bass as bass
import concourse.mybir as mybir
from concourse.bass_utils import run_bass_kernel_spmd

F32 = mybir.dt.float32
BF16 = mybir.dt.bfloat16
AF = mybir.ActivationFunctionType
ALU = mybir.AluOpType

D = 2048
KC = 16
LAT = 2048
CTX = 256
LT = LAT + CTX
NT = LT // 128
DFF = 5632
FC = DFF // 128
INW = 6144
EPS = 1e-6
NEG = -30000.0
CAST_MODE = 1
NORM_STEPS = 8


class Buf:
    __slots__ = ("name", "w", "r", "dsem", "excl")

    def __init__(self, name="", excl=False):
        self.excl = excl
        self.name = name
        self.w = None
        self.r = {}
        self.dsem = None


class _Eng:
    def __init__(self, name):
        self.name = name
        self.ops = []
        self.waited = {}
        self.own = set()
        self.sem = None
        self.cnt = 0
        self.pending = False


class Sched:
    LIMIT = 30000

    def __init__(self, nc, stack, nsem=90):
        self.nc = nc
        self.sems = [stack.enter_context(nc.semaphore(f"sm{i}")) for i in range(nsem)]
        self.eng_free = list(range(16))[::-1]
        self.free = list(range(16, nsem))[::-1]
        self.semtot = [0] * nsem
        self.dma_sems = set()
        self.nobar = set()
        self.E = {n: _Eng(n) for n in ("pe", "act", "dve", "pool", "sp")}
        self.stage_bufs = []
        self.dead = False

    def newsem(self):
        return self.free.pop()

    def buf(self, name=""):
        b = Buf(name)
        self.stage_bufs.append(b)
        return b

    def bufs(self, n, name=""):
        return [self.buf(f"{name}{i}") for i in range(n)]

    def _eng_token(self, E, signal):
        if E.sem is None or (signal and E.cnt >= self.LIMIT and not E.pending):
            E.sem = self.eng_free.pop()
            E.own.add(E.sem)
            E.cnt = 0
        tok = (E.sem, E.cnt + 1)
        if signal:
            E.cnt += 1
            E.pending = False
            self.semtot[E.sem] = E.cnt
        else:
            E.pending = True
        return tok

    @staticmethod
    def _deps(reads, writes):
        d = []
        for b in reads:
            if b.w is not None:
                d.append(b.w)
        for b in writes:
            if b.w is not None:
                d.append(b.w)
            d.extend(b.r.items())
        return d

    @staticmethod
    def _need(E, deps):
        out = {}
        for si, v in deps:
            if E.name == "pe" and si in E.own:
                continue
            if E.waited.get(si, 0) >= v:
                continue
            if out.get(si, 0) < v:
                out[si] = v
        E.waited.update(out)
        return list(out.items())

    @staticmethod
    def _mark(tok, reads, writes):
        si, v = tok
        for b in reads:
            if b.r.get(si, 0) < v:
                b.r[si] = v
        for b in writes:
            b.w = tok
            b.r = {}

    def op(self, eng, fn, reads=(), writes=(), signal=True):
        if self.dead:
            return
        if any(b.excl for b in reads):
            writes = list(writes) + [b for b in reads if b.excl]
            reads = [b for b in reads if not b.excl]
        E = self.E[eng]
        waits = self._need(E, self._deps(reads, writes))
        tok = self._eng_token(E, signal)
        E.ops.append((waits, fn, tok if signal else None, 1))
        self._mark(tok, reads, writes)

    def dma(self, eng, fn, sb=None, reads=(), writes=(), sem=None):
        if self.dead:
            return
        E = self.E[eng]
        waits = self._need(E, self._deps(reads, writes))
        if sem is None:
            if sb.dsem is None:
                sb.dsem = self.newsem()
            sem = sb.dsem
        self.semtot[sem] += 16
        tok = (sem, self.semtot[sem])
        self.dma_sems.add(sem)
        E.ops.append((waits, fn, tok, 16))
        self._mark(tok, reads, writes)

    def barrier(self, final=False):
        if self.dead:
            return
        toks = []
        for E in self.E.values():
            assert not E.pending, E.name
            for si in E.own:
                if self.semtot[si] > 0:
                    toks.append((si, self.semtot[si]))
        for si in self.dma_sems:
            if si in self.nobar and not final:
                continue
            toks.append((si, self.semtot[si]))
        for E in self.E.values():
            waits = self._need(E, toks)
            if waits:
                E.ops.append((waits, None, None, 0))

    def end_stage(self):
        self.barrier()
        for b in self.stage_bufs:
            if b.dsem is not None and b.dsem not in self.nobar:
                self.free.append(b.dsem)
                b.dsem = None
        self.stage_bufs = []

    def emit(self, block):
        sems = self.sems

        def rep(name, e):
            for waits, fn, tok, inc in self.E[name].ops:
                for si, v in waits:
                    e.wait_ge(sems[si], v)
                if fn is not None:
                    ins = fn(e)
                    if tok is not None:
                        ins.then_inc(sems[tok[0]], inc)

        @block.tensor
        def _(e):
            rep("pe", e)

        @block.scalar
        def _(e):
            rep("act", e)

        @block.vector
        def _(e):
            rep("dve", e)

        @block.gpsimd
        def _(e):
            rep("pool", e)

        @block.sync
        def _(e):
            rep("sp", e)


class _Stop(Exception):
    pass


def build(debug=False, stop=None):
    nc = bass.Bass("TRN2", target_bir_lowering=False)
    dk = "ExternalOutput" if debug else "Internal"

    def din(name, shape, dt=F32):
        return nc.dram_tensor(name, list(shape), dt, kind="ExternalInput").ap()

    x2 = din("x2", [2, LAT, D])
    ctx2 = din("ctx2", [2, CTX, D])
    cT_d = din("cT", [128, KC, 3])
    adaw = din("ada_w", [D, 6 * D])
    adab_d = din("ada_bT", [128, 96])
    adab3_d = din("ada_b3", [3, 6 * D])
    gT_d = din("gT", [128, 4, KC])
    w_in = din("w_in", [D, INW])
    w_out = din("w_out", [D, D])
    w_gate = din("w_gate", [D, DFF])
    w_up = din("w_up", [D, DFF])
    w_down = din("w_down", [DFF, D])
    nab_d = din("nab", [8, 128, 14, 256])
    lgr_d = din("lgr", [128, 16])
    ropeC_d = din("ropeC", [128, 16, 64])
    ropeS_d = din("ropeS", [128, 16, 64])
    ident_d = din("ident", [128, 128])
    rcon_d = din("rcon", [128, 5, 128])
    pidx_d = din("pidx", [128, 4])
    out2 = nc.dram_tensor("out2", [2, LAT, D], F32, kind="ExternalOutput").ap()

    winb = nc.dram_tensor("winb", [D, INW], BF16).ap()
    woutb = nc.dram_tensor("woutb", [D, D], BF16).ap()
    wgb = nc.dram_tensor("wgb", [D, DFF], BF16).ap()
    wub = nc.dram_tensor("wub", [D, DFF], BF16).ap()
    wdb = nc.dram_tensor("wdb", [DFF, D], BF16).ap()
    QTs = nc.dram_tensor("QTs", [8, 128, LAT], BF16, kind=dk).ap()
    KTs = nc.dram_tensor("KTs", [8, 128, LT], BF16, kind=dk).ap()
    PT = nc.dram_tensor("PT", [8, LT, 512], BF16, kind=dk).ap()
    xnew = nc.dram_tensor("xnew", [2, LAT, D], F32, kind=dk).ap()
    gvec = nc.dram_tensor("gvec", [4, D], F32).ap()
    mixdbg = nc.dram_tensor("mixdbg", [128, KC, LAT], BF16, kind=dk).ap() if debug else None
    statdbg = nc.dram_tensor("statdbg", [128, NT, 4], F32, kind=dk).ap() if debug else None
    hTdbg = nc.dram_tensor("hTdbg", [128, KC, LT], BF16, kind=dk).ap() if debug else None

    with ExitStack() as top:
        S = Sched(nc, top)

        _uid = [0]

        def sb(stack, name, shape, dt):
            _uid[0] += 1
            return stack.enter_context(nc.sbuf_tensor(f"{name}_{_uid[0]}", list(shape), dt))

        ident = sb(top, "identb", [128, 128], BF16)
        AB = sb(top, "AB", [128, 10, KC], F32)
        lgr = sb(top, "lgr", [128, 16], F32)
        dtab = sb(top, "dtab", [128, 4, 8], F32)
        gC = sb(top, "gC", [128, 16], F32)
        retmask = sb(top, "retmask", [128, 8, 128], BF16)
        neghalf = sb(top, "neghalf", [128, 4], F32)
        dmy = sb(top, "dmy", [128, 4], F32)
        b_const = Buf("const")
        bankB = [Buf(f"bank{i}", excl=True) for i in range(8)]

        ps = [top.enter_context(nc.psum_tensor(f"ps{i}", [128, 512], F32)) for i in range(8)]

        def act_accum(fn, reads, writes):
            S.op("act", fn, reads=reads, writes=writes, signal=False)
            S.op("act", lambda e: e.activation(out=dmy[:, 1:2], in_=dmy[:, 0:1], func=AF.Copy))

        def psb(i):
            return ps[i][:].bitcast(BF16)

        def chk(tag):
            if stop == tag and not S.dead:
                S.barrier(final=True)
                S.dead = True

        with ExitStack() as st:
            wbufs = {}
            for name, src, dst, rows, last in (("win", w_in, winb, D, 8192),):
                b = Buf(name)
                sem = S.newsem()
                S.nobar.add(sem)
                nblk = 8
                rb = rows // nblk
                for i in range(nblk):
                    S.dma(
                        "pool",
                        lambda e, src=src, dst=dst, i=i, rb=rb, last=last: e.dma_start(
                            out=dst[i * rb:(i + 1) * rb, :], in_=src[i * rb:(i + 1) * rb, :],
                            max_dma_last_dim=last),
                        sem=sem, writes=[b] if i == nblk - 1 else [],
                    )
                wbufs[name] = b
            for name in ("wout", "wg", "wu", "wd"):
                wbufs[name] = Buf(name)
            cast_jobs = []
            for src, dst, rows, cols in ((w_out, woutb, D, D), (w_gate, wgb, D, DFF), (w_up, wub, D, DFF), (w_down, wdb, DFF, D)):
                for r0 in range(0, rows, 128):
                    for c0 in range(0, cols, 2048):
                        cw = min(2048, cols - c0)
                        cast_jobs.append((src[r0:r0 + 128, c0:c0 + cw], dst[r0:r0 + 128, c0:c0 + cw], cw))

            identf = sb(st, "identf", [128, 128], F32)
            cT = sb(st, "cT", [128, KC, 3], F32)
            sT = sb(st, "sT", [128, KC, 3], F32)
            adab = sb(st, "adab", [128, 96], F32)
            gT = sb(st, "gT", [128, 4, KC], F32)
            modT = sb(st, "modT", [128, 96, 3], F32)
            gtmp = sb(st, "gtmp", [128, 4, KC], F32)
            rcon = sb(st, "rcon", [128, 5, 128], F32)
            pidx = sb(st, "pidx", [128, 4], F32)
            tmp4 = sb(st, "tmp4", [128, 4, 8], F32)
            e1 = sb(st, "e1", [128, 128], F32)
            e2 = sb(st, "e2", [128, 128], F32)
            m1 = sb(st, "m1", [128, 128], F32)
            aw = [sb(st, f"aw{i}", [128, KC, 512], F32) for i in range(2)]
            b_aw = S.bufs(2, "aw")
            b_ld = S.buf("ld0")
            b_ps0 = S.buf("ps0")
            b_mod = S.buf("mod")
            b_tmp = S.buf("tmp")

            for dst, src in ((identf, ident_d), (cT, cT_d), (adab, adab_d), (gT, gT_d), (lgr, lgr_d),
                             (rcon, rcon_d), (pidx, pidx_d)):
                bb = S.buf("l")
                S.dma("sp", lambda e, dst=dst, src=src: e.dma_start(out=dst[:], in_=src), sb=bb, writes=[bb])
            loads_done = list(S.stage_bufs[-7:])
            S.op("dve", lambda e: e.memset(neghalf[:], -0.5), writes=[b_const])
            S.op("dve", lambda e: e.memset(dmy[:], 0.0), writes=[b_const])
            S.op("act", lambda e: e.activation(out=ident[:], in_=identf[:], func=AF.Copy), reads=loads_done, writes=[b_const])
            S.op("act", lambda e: e.activation(out=sT[:], in_=cT[:], func=AF.Silu), reads=loads_done, writes=[b_tmp])

            mrow = sb(st, "mrow", [4, 6 * D], F32)
            adab3 = sb(st, "adab3", [4, 6 * D], F32)
            b_mrow = S.buf("mrow")
            b_ab3 = S.buf("ab3")
            b_pa = S.bufs(4, "pa")
            S.dma("sp", lambda e: e.dma_start(out=adab3[0:3, :], in_=adab3_d), sb=b_ab3, writes=[b_ab3])
            for g in range(24):
                t = g % 2
                bk = 1 + g % 4
                S.dma("sp", lambda e, g=g, t=t: e.dma_start(
                    out=aw[t][:], in_=adaw[:, g * 512:(g + 1) * 512].rearrange("(j p) c -> p j c", p=128)),
                    sb=b_aw[t], writes=[b_aw[t]])
                for j in range(KC):
                    S.op("pe", lambda e, t=t, j=j, bk=bk: e.matmul(
                        ps[bk][0:3, :], lhsT=sT[:, j, :], rhs=aw[t][:, j, :], start=(j == 0), stop=(j == KC - 1)),
                        reads=[b_aw[t], b_tmp], writes=[b_pa[bk - 1]], signal=(j == KC - 1))
                S.op("dve", lambda e, g=g, bk=bk: e.tensor_tensor(
                    out=mrow[0:3, g * 512:(g + 1) * 512], in0=ps[bk][0:3, :], in1=adab3[0:3, g * 512:(g + 1) * 512], op=ALU.add),
                    reads=[b_pa[bk - 1], b_ab3], writes=[b_mrow])
            for ch in range(96):
                S.op("pe", lambda e, ch=ch: e.transpose(
                    out=ps[0][:, ch * 3:ch * 3 + 3], in_=mrow[0:3, ch * 128:(ch + 1) * 128], identity=identf[0:3, 0:3]),
                    reads=[b_mrow] + loads_done, writes=[b_ps0], signal=(ch == 95))
            S.op("dve", lambda e: e.tensor_copy(out=modT[:].rearrange("p c b -> p (c b)"), in_=ps[0][:, 0:288]),
                 reads=[b_ps0], writes=[b_mod])
            for b in range(3):
                S.op("dve", lambda e, b=b: e.scalar_tensor_tensor(
                    out=AB[:, b, :], in0=modT[:, 16:32, b], scalar=1.0, in1=gT[:, 0, :], op0=ALU.add, op1=ALU.mult),
                    reads=[b_mod], writes=[b_const])
                S.op("dve", lambda e, b=b: e.tensor_copy(out=AB[:, 3 + b, :], in_=modT[:, 0:16, b]),
                     reads=[b_mod], writes=[b_const])
            for b in range(2):
                S.op("dve", lambda e, b=b: e.scalar_tensor_tensor(
                    out=AB[:, 6 + b, :], in0=modT[:, 64:80, b], scalar=1.0, in1=gT[:, 2, :], op0=ALU.add, op1=ALU.mult),
                    reads=[b_mod], writes=[b_const])
                S.op("dve", lambda e, b=b: e.tensor_copy(out=AB[:, 8 + b, :], in_=modT[:, 48:64, b]),
                     reads=[b_mod], writes=[b_const])
                S.op("dve", lambda e, b=b: e.tensor_tensor(out=gtmp[:, b, :], in0=modT[:, 32:48, b], in1=gT[:, 1, :], op=ALU.mult),
                     reads=[b_mod], writes=[b_tmp])
                S.op("dve", lambda e, b=b: e.tensor_tensor(out=gtmp[:, 2 + b, :], in0=modT[:, 80:96, b], in1=gT[:, 3, :], op=ALU.mult),
                     reads=[b_mod], writes=[b_tmp])
            b_gv = S.buf("gv")
            S.dma("sp", lambda e: e.dma_start(out=gvec.rearrange("g (j p) -> p g j", p=128), in_=gtmp[:],
                                              allow_slow_non_contiguous=True), sb=b_gv, reads=[b_tmp], writes=[b_gv])

            for k in range(4):
                off = 0 if k in (0, 2) else 8
                S.op("dve", lambda e, k=k, off=off: e.tensor_scalar(
                    out=tmp4[:, k, :], in0=lgr[:, off:off + 8], scalar1=pidx[:, k:k + 1], scalar2=None, op0=ALU.mult),
                    reads=loads_done, writes=[b_tmp])
            S.op("act", lambda e: e.activation(out=dtab[:], in_=tmp4[:], func=AF.Exp), reads=[b_tmp], writes=[b_const])
            S.op("dve", lambda e: e.tensor_scalar(out=dtab[:, 2:4, :], in0=dtab[:, 2:4, :], scalar1=0.125, scalar2=None, op0=ALU.mult),
                 reads=[b_const], writes=[b_const])
            S.op("act", lambda e: e.activation(out=gC[:], in_=lgr[:], func=AF.Exp, scale=128.0), reads=loads_done, writes=[b_const])
            for h in range(8):
                S.op("act", lambda e, h=h: e.activation(out=e1[:], in_=rcon[:, 0, :], func=AF.Exp, scale=lgr[:, h:h + 1]),
                     reads=loads_done, writes=[b_tmp])
                S.op("act", lambda e, h=h: e.activation(out=e2[:], in_=rcon[:, 1, :], func=AF.Exp, scale=lgr[:, 8 + h:9 + h]),
                     reads=loads_done, writes=[b_tmp])
                S.op("dve", lambda e: e.tensor_tensor(out=m1[:], in0=e1[:], in1=rcon[:, 2, :], op=ALU.mult), reads=[b_tmp], writes=[b_tmp])
                S.op("dve", lambda e: e.tensor_tensor(out=e2[:], in0=e2[:], in1=rcon[:, 3, :], op=ALU.mult), reads=[b_tmp], writes=[b_tmp])
                S.op("dve", lambda e: e.tensor_tensor(out=m1[:], in0=m1[:], in1=e2[:], op=ALU.add), reads=[b_tmp], writes=[b_tmp])
                S.op("dve", lambda e, h=h: e.tensor_tensor(out=retmask[:, h, :], in0=m1[:], in1=rcon[:, 4, :], op=ALU.add),
                     reads=[b_tmp], writes=[b_const, b_tmp])
            S.end_stage()
            chk("S0" + ("" if "S0" == "S0" else str(s)))

        def make_norm(st, tag, hT, banks, bank_bufs, xin_ext=None):
            if xin_ext is None:
                xin = [sb(st, f"{tag}xin{i}", [128, D], F32) for i in range(2)]
                b_x = S.bufs(2, "x")
            else:
                xin, b_x = xin_ext
            yb = [sb(st, f"{tag}y{i}", [128, D], BF16) for i in range(2)]
            stat = [sb(st, f"{tag}stat{i}", [128, 4], F32) for i in range(2)]
            bstn = [sb(st, f"{tag}bstn{i}", [128, 4, 6], F32) for i in range(2)]
            b_y = S.bufs(2, "y")
            b_s = S.bufs(2, "s")
            state = {"n": 0, "l": 0, "s": 0}

            def load(src):
                t = state["l"] % 2
                state["l"] += 1
                S.dma("sp", lambda e: e.dma_start(out=xin[t][:], in_=src), sb=b_x[t], writes=[b_x[t]])

            def stats():
                t = state["s"] % 2
                state["s"] += 1
                for q in range(4):
                    S.op("dve", lambda e, q=q: e.bn_stats(out=bstn[t][:, q, :], in_=xin[t][:, q * 512:(q + 1) * 512]),
                         reads=[b_x[t]], writes=[b_s[t]])
                S.op("dve", lambda e: e.bn_aggr(out=stat[t][:, 0:2], in_=bstn[t][:].rearrange("p q s -> p (q s)")),
                     reads=[b_s[t]], writes=[b_s[t]])
                S.op("dve", lambda e: e.scalar_tensor_tensor(out=stat[t][:, 2:3], in0=stat[t][:, 0:1], scalar=stat[t][:, 0:1],
                                                             in1=stat[t][:, 1:2], op0=ALU.mult, op1=ALU.add),
                     reads=[b_s[t]], writes=[b_s[t]])
                S.op("dve", lambda e: e.tensor_scalar(out=stat[t][:, 2:3], in0=stat[t][:, 2:3], scalar1=EPS, scalar2=None, op0=ALU.add),
                     reads=[b_s[t]], writes=[b_s[t]])
                S.op("act", lambda e: e.activation(out=stat[t][:, 2:3], in_=stat[t][:, 2:3], func=AF.Sqrt),
                     reads=[b_s[t]], writes=[b_s[t]])
                S.op("dve", lambda e: e.reciprocal(out=stat[t][:, 3:4], in_=stat[t][:, 2:3]),
                     reads=[b_s[t]], writes=[b_s[t]])
                S.op("act", lambda e: e.activation(out=yb[t][:], in_=xin[t][:], func=AF.Copy, scale=stat[t][:, 3:4]),
                     reads=[b_x[t], b_s[t]], writes=[b_y[t]])

            def xpose(ai, bi, c0, hbj):
                t = state["n"] % 2
                state["n"] += 1
                for half in range(2):
                    bk = banks[t * 2 + half]
                    bb = bank_bufs[t * 2 + half]
                    for jj in range(8):
                        j = half * 8 + jj
                        S.op("pe", lambda e, bk=bk, jj=jj, j=j: e.transpose(
                            out=psb(bk)[:, jj * 128:(jj + 1) * 128], in_=yb[t][:, j * 128:(j + 1) * 128], identity=ident[:]),
                            reads=[b_y[t], b_const], writes=[bb], signal=(jj == 7))
                    for jj in range(8):
                        j = half * 8 + jj
                        S.op("dve", lambda e, bk=bk, jj=jj, j=j: e.tensor_scalar(
                            out=hT[:, j, c0:c0 + 128], in0=psb(bk)[:, jj * 128:(jj + 1) * 128],
                            scalar1=AB[:, ai, j:j + 1], scalar2=AB[:, bi, j:j + 1], op0=ALU.mult, op1=ALU.add),
                            reads=[bb, b_const], writes=[hbj[j]])
            def tile(src, ai, bi, c0, hbj):
                if state["l"] == state["n"]:
                    load(src)
                if state["s"] == state["n"]:
                    stats()
                xpose(ai, bi, c0, hbj)

            tile.load = load
            tile.stats = stats
            tile.xpose = xpose
            return tile

        cpos = {"next": 0}

        def make_caster(stk, active, nbuf=2, both=False, limit=None):
            if not active:
                return (lambda n=1: None), (lambda all_jobs=False: None)
            cin = [sb(stk, f"cin{i}", [128, 2048], F32) for i in range(nbuf)]
            cout = [sb(stk, f"cout{i}", [128, 2048], BF16) for i in range(nbuf)]
            b_cin = S.bufs(nbuf, "cin")
            b_cout = S.bufs(nbuf, "cout")
            pend = []
            njobs = len(cast_jobs) if limit is None else limit

            def load():
                k = cpos["next"]
                if k >= njobs:
                    return
                cpos["next"] = k + 1
                srcap, dstap, cw = cast_jobs[k]
                u = k % nbuf
                S.dma("sp", lambda e: e.dma_start(out=cin[u][:, 0:cw], in_=srcap), sb=b_cin[u], writes=[b_cin[u]])
                pend.append((u, dstap, cw))

            def finish():
                if not pend:
                    return
                u, dstap, cw = pend.pop(0)
                if both and u % 2 == 1:
                    S.op("dve", lambda e: e.tensor_copy(out=cout[u][:, 0:cw], in_=cin[u][:, 0:cw]),
                         reads=[b_cin[u]], writes=[b_cout[u]])
                else:
                    S.op("act", lambda e: e.activation(out=cout[u][:, 0:cw], in_=cin[u][:, 0:cw], func=AF.Copy),
                         reads=[b_cin[u]], writes=[b_cout[u]])
                S.dma("sp", lambda e: e.dma_start(out=dstap, in_=cout[u][:, 0:cw]), sb=b_cout[u], reads=[b_cout[u]])

            def step(n=1):
                if CAST_MODE == 0:
                    return
                for _ in range(n):
                    if len(pend) >= nbuf or cpos["next"] >= njobs:
                        finish()
                    load()

            def drain(all_jobs=False):
                while pend:
                    finish()
                if all_jobs:
                    while cpos["next"] < njobs:
                        load()
                        finish()
            return step, drain

        for s in range(2):
            with ExitStack() as st:
                hT = sb(st, "hT", [128, KC, LT], BF16)
                hb = {(j, ti): S.buf("h") for j in range(KC) for ti in range(NT)}
                srcs = [ctx2[s, ti * 128:(ti + 1) * 128, :] for ti in range(2)] + \
                       [x2[s, ti * 128:(ti + 1) * 128, :] for ti in range(16)]
                a_idx = [2, 2] + [s] * 16
                b_idx = [5, 5] + [3 + s] * 16
                wb = [sb(st, f"wb{i}", [128, KC, 512], BF16) for i in range(2)]
                b_wb = S.bufs(2, "wb")
                stg = [sb(st, f"stg{i}", [128, 512], BF16) for i in range(4)]
                b_stg = S.bufs(4, "stg")
                b_pp = S.bufs(4, "pp")
                cnt = 0

                def load_wb(cg, t):
                    S.dma("sp", lambda e: e.dma_start(
                        out=wb[t][:], in_=winb[:, cg * 512:(cg + 1) * 512].rearrange("(j p) c -> p j c", p=128)),
                        sb=b_wb[t], reads=[wbufs["win"]], writes=[b_wb[t]])

                def proj_tm(cg, t, ti):
                    nonlocal cnt
                    is_gate = cg in (10, 11)
                    bk = 4 + cnt % 4
                    si = cnt % 4
                    cnt += 1
                    for j in range(KC):
                        S.op("pe", lambda e, j=j: e.matmul(
                            ps[bk][:], lhsT=hT[:, j, ti * 128:(ti + 1) * 128], rhs=wb[t][:, j, :],
                            start=(j == 0), stop=(j == KC - 1)),
                            reads=[b_wb[t], hb[(j, ti)]], writes=[b_pp[bk - 4]], signal=(j == KC - 1))
                    if is_gate:
                        S.op("act", lambda e: e.activation(out=stg[si][:], in_=ps[bk][:], func=AF.Silu),
                             reads=[b_pp[bk - 4]], writes=[b_stg[si]])
                    elif cnt % 2 == 0 or cg in (4, 5):
                        S.op("act", lambda e: e.activation(out=stg[si][:], in_=ps[bk][:], func=AF.Copy),
                             reads=[b_pp[bk - 4]], writes=[b_stg[si]])
                    else:
                        S.op("dve", lambda e: e.tensor_copy(out=stg[si][:], in_=ps[bk][:]),
                             reads=[b_pp[bk - 4]], writes=[b_stg[si]])
                    S.dma("sp", lambda e: e.dma_start(out=PT[cg - 4, ti * 128:(ti + 1) * 128, :], in_=stg[si][:]),
                          sb=b_stg[si], reads=[b_stg[si]])
                    if cg not in (4, 5):
                        cast1_step(1)

                ntile = make_norm(st, f"a{s}", hT, [0, 1, 2, 3], S.bufs(4, "pt"))
                cast1_step, cast1_drain = make_caster(st, s == 1, nbuf=4, both=True)
                load_wb(4, 0)
                load_wb(5, 1)
                ntile.load(srcs[0])
                ntile.load(srcs[1])
                ntile.stats()
                for ti in range(NT):
                    if ti + 1 < NT:
                        ntile.stats()
                    if ti + 2 < NT:
                        ntile.load(srcs[ti + 2])
                    ntile.xpose(a_idx[ti], b_idx[ti], ti * 128, [hb[(j, ti)] for j in range(KC)])
                    if ti >= 1:
                        proj_tm(4, 0, ti - 1)
                        proj_tm(5, 1, ti - 1)
                proj_tm(4, 0, NT - 1)
                proj_tm(5, 1, NT - 1)
                if debug and s == 0:
                    bd = S.buf("dbgh")
                    S.dma("sp", lambda e: e.dma_start(out=hTdbg, in_=hT[:]), sb=bd, reads=list(hb.values()))
                chk(f"S1n{s}")
                for k, cg in enumerate([0, 1, 2, 3, 6, 7, 8, 9, 10, 11]):
                    t = k % 2
                    if cg == 6:
                        chk(f"S1f{s}")
                    load_wb(cg, t)
                    if cg < 4:
                        isq = cg < 2
                        for hc in range(4):
                            head = (cg % 2) * 4 + hc
                            groups = [(256 + g * 512, 512) for g in range(4)]
                            if not isq:
                                groups = [(0, 256)] + groups
                            for (c0, n) in groups:
                                bk = 4 + cnt % 4
                                si = cnt % 4
                                cnt += 1
                                tiles = range(c0 // 128, (c0 + n) // 128)
                                for j in range(KC):
                                    S.op("pe", lambda e, bk=bk, t=t, j=j, hc=hc, c0=c0, n=n: e.matmul(
                                        ps[bk][:, 0:n], lhsT=wb[t][:, j, hc * 128:(hc + 1) * 128], rhs=hT[:, j, c0:c0 + n],
                                        start=(j == 0), stop=(j == KC - 1)),
                                        reads=[b_wb[t]] + [hb[(j, ti)] for ti in tiles], writes=[b_pp[bk - 4]],
                                        signal=(j == KC - 1))
                                if cnt % 2 == 0:
                                    S.op("act", lambda e, bk=bk, si=si, n=n: e.activation(out=stg[si][:, 0:n], in_=ps[bk][:, 0:n], func=AF.Copy),
                                         reads=[b_pp[bk - 4]], writes=[b_stg[si]])
                                else:
                                    S.op("dve", lambda e, bk=bk, si=si, n=n: e.tensor_copy(out=stg[si][:, 0:n], in_=ps[bk][:, 0:n]),
                                         reads=[b_pp[bk - 4]], writes=[b_stg[si]])
                                if isq:
                                    dst = QTs[head, :, c0 - 256:c0 - 256 + n]
                                else:
                                    dst = KTs[head, :, c0:c0 + n]
                                S.dma("sp", lambda e, dst=dst, si=si, n=n: e.dma_start(out=dst, in_=stg[si][:, 0:n]),
                                      sb=b_stg[si], reads=[b_stg[si]])
                    else:
                        need_ctx = cg in (4, 5, 7, 8, 9)
                        for ti in range(0 if need_ctx else 2, NT):
                            proj_tm(cg, t, ti)
                cast1_drain(True)
                S.barrier(final=True)
                S.end_stage()
                chk("S1" + ("" if "S1" == "S0" else str(s)))

            with ExitStack() as stm:
                mixT = sb(stm, "mixT", [128, KC, LAT], BF16)
                b_mix = {(j, c): Buf("mix") for j in range(KC) for c in range(16)}

                with ExitStack() as st:
                    Kr = sb(st, "Kr", [128, NT, 512], BF16)
                    QTr = sb(st, "QTr", [128, 4, LAT], BF16)
                    KTr = sb(st, "KTr", [128, 4, LAT], BF16)
                    b_Kr = S.buf("Kr")
                    b_QT = S.bufs(16, "QT")
                    b_KT = S.bufs(16, "KT")
                    hbk = [bankB[i // 2] for i in range(16)]

                    def hbank(i, n=256):
                        return ps[i // 2][:, (i % 2) * 256:(i % 2) * 256 + n]

                    def hbank_bf(i, n=256):
                        return psb(i // 2)[:, (i % 2) * 512:(i % 2) * 512 + n]

                    with ExitStack() as st2:
                        Qr = sb(st2, "Qr", [128, 16, 512], BF16)
                        rC = sb(st2, "rC", [128, 16, 64], F32)
                        rS = sb(st2, "rS", [128, 16, 64], F32)
                        t1 = [sb(st2, f"t1_{i}", [128, 512], F32) for i in range(2)]
                        t2 = [sb(st2, f"t2_{i}", [128, 512], F32) for i in range(2)]
                        b_Qr = S.buf("Qr")
                        b_rt = S.buf("rt")
                        b_t1 = S.bufs(2, "t1")
                        b_t2 = S.bufs(2, "t2")
                        b_Xt = {}
                        S.dma("sp", lambda e: e.dma_start(out=Qr[:], in_=PT[2, 256:, :].rearrange("(t p) c -> p t c", p=128)),
                              sb=b_Qr, writes=[b_Qr])
                        S.dma("sp", lambda e: e.dma_start(out=Kr[:], in_=PT[3, :, :].rearrange("(t p) c -> p t c", p=128)),
                              sb=b_Kr, writes=[b_Kr])
                        b_rc = S.buf("rc")
                        S.dma("sp", lambda e: e.dma_start(out=rC[:], in_=ropeC_d), sb=b_rc, writes=[b_rc])
                        b_rs = S.buf("rs")
                        S.dma("sp", lambda e: e.dma_start(out=rS[:], in_=ropeS_d), sb=b_rs, writes=[b_rs])
                        k = 0
                        for lt in range(16):
                            for which in range(2):
                                X = Qr[:, lt, :] if which == 0 else Kr[:, 2 + lt, :]
                                bX = S.buf("X")
                                b_Xt[(which, lt)] = bX
                                base = b_Qr if which == 0 else b_Kr
                                u = k % 2
                                k += 1
                                xv = X.rearrange("p (h d) -> p h d", h=8)
                                xs = X.rearrange("p (h a b c) -> p h a b c", h=8, a=2, b=2)
                                Cb = rC[:, lt, :].unsqueeze(1).to_broadcast([128, 8, 64])
                                Sb5 = rS[:, lt, :].rearrange("p (a b c) -> p a b c", a=2, b=2).unsqueeze(1).to_broadcast([128, 8, 2, 2, 16])
                                t1v = t1[u][:].rearrange("p (h d) -> p h d", h=8)
                                t2v = t2[u][:].rearrange("p (h a b c) -> p h a b c", h=8, a=2, b=2)
                                S.op("dve", lambda e, t1v=t1v, xv=xv, Cb=Cb: e.tensor_tensor(out=t1v, in0=xv, in1=Cb, op=ALU.mult),
                                     reads=[base, b_rc], writes=[b_t1[u]])
                                S.op("dve", lambda e, t2v=t2v, xs=xs, Sb5=Sb5: e.tensor_tensor(
                                    out=t2v[:, :, :, 0, :], in0=xs[:, :, :, 1, :], in1=Sb5[:, :, :, 0, :], op=ALU.mult),
                                    reads=[base, b_rs], writes=[b_t2[u]])
                                S.op("dve", lambda e, t2v=t2v, xs=xs, Sb5=Sb5: e.tensor_tensor(
                                    out=t2v[:, :, :, 1, :], in0=xs[:, :, :, 0, :], in1=Sb5[:, :, :, 1, :], op=ALU.mult),
                                    reads=[base, b_rs], writes=[b_t2[u]])
                                S.op("dve", lambda e, X=X, u=u: e.tensor_tensor(out=X, in0=t1[u][:], in1=t2[u][:], op=ALU.add),
                                     reads=[b_t1[u], b_t2[u], base], writes=[bX])
                                hbi = (k % 4)
                                for pr in range(4):
                                    S.op("pe", lambda e, X=X, pr=pr, hbi=hbi: e.transpose(
                                        out=hbank_bf(hbi, 512)[:, pr * 128:(pr + 1) * 128], in_=X[:, pr * 128:(pr + 1) * 128], identity=ident[:]),
                                        reads=[bX, b_const], writes=[hbk[hbi]], signal=(pr == 3))
                                dstT = QTr if which == 0 else KTr
                                bT = b_QT[lt] if which == 0 else b_KT[lt]
                                S.op("act" if which == 0 else "dve",
                                     (lambda e, dstT=dstT, lt=lt, hbi=hbi: e.activation(
                                         out=dstT[:, :, lt * 128:(lt + 1) * 128], in_=hbank_bf(hbi, 512).rearrange("p (a t) -> p a t", a=4), func=AF.Copy))
                                     if which == 0 else
                                     (lambda e, dstT=dstT, lt=lt, hbi=hbi: e.tensor_copy(
                                         out=dstT[:, :, lt * 128:(lt + 1) * 128], in_=hbank_bf(hbi, 512).rearrange("p (a t) -> p a t", a=4))),
                                     reads=[hbk[hbi]], writes=[bT])
                        kr_all = [b_Kr] + [b_Xt[(1, lt)] for lt in range(16)]
                        S.barrier()
                    Vh = [sb(st, f"Vh{i}", [128, NT, 128], BF16) for i in range(2)]
                    Gh = [sb(st, f"Gh{i}", [128, 16, 128], BF16) for i in range(2)]
                    b_Vh = S.bufs(2, "Vh")
                    b_Gh = S.bufs(2, "Gh")
                    Sst = [sb(st, f"Sst{i}", [128, 16, 128], F32) for i in range(2)]
                    Sbf = [sb(st, f"Sbf{i}", [128, 16, 128], BF16) for i in range(2)]
                    tmpS = sb(st, "tmpS", [128, 2, 128], F32)
                    b_Sst = S.bufs(2, "Sst")
                    b_Sbf = S.bufs(2, "Sbf")
                    b_tmpS = S.buf("tmpS")
                    kdh = [sb(st, f"kdh{i}", [128, NT, 64], BF16) for i in range(2)]
                    b_kdh = S.bufs(2, "kdh")
                    PTt = [sb(st, f"PTt{i}", [128, 2, 128], BF16) for i in range(2)]
                    b_PTt = S.bufs(2, "PTt")
                    o1 = [sb(st, f"o1_{i}", [128, 256], F32) for i in range(2)]
                    o2 = [sb(st, f"o2_{i}", [128, 256], F32) for i in range(2)]
                    o3 = [sb(st, f"o3_{i}", [128, 256], F32) for i in range(2)]
                    nrm = [sb(st, f"nrm{i}", [128, 256], F32) for i in range(2)]
                    retm = [sb(st, f"retm{i}", [128, 2, 128], BF16) for i in range(2)]
                    bst = [sb(st, f"bst{i}", [128, 2, 6], F32) for i in range(2)]
                    mv = [sb(st, f"mv{i}", [128, 2, 2], F32) for i in range(2)]
                    rs2 = [sb(st, f"rs2{i}", [128, 4], F32) for i in range(2)]
                    b_o = S.bufs(2, "o")
                    b_st = S.bufs(2, "st")
                    b_nrm = S.bufs(2, "nrm")
                    b_retm = S.bufs(2, "retm")
                    ucnt = 0
                    b_U = [bankB[i % 2] for i in range(8)]
                    cast_step, cast_drain = make_caster(st, s == 0, limit=64)
                    for h in range(8):
                        hp = h % 2
                        pr0 = hp * 64
                        pair = h // 2
                        S.dma("sp", lambda e, h=h, hp=hp: e.dma_start(
                            out=Vh[hp][:], in_=PT[4 + h // 4, :, (h % 4) * 128:(h % 4 + 1) * 128].rearrange("(t p) c -> p t c", p=128)),
                            sb=b_Vh[hp], writes=[b_Vh[hp]])
                        S.dma("sp", lambda e, h=h, hp=hp: e.dma_start(
                            out=Gh[hp][:], in_=PT[6 + h // 4, 256:, (h % 4) * 128:(h % 4 + 1) * 128].rearrange("(t p) c -> p t c", p=128)),
                            sb=b_Gh[hp], writes=[b_Gh[hp]])
                        for dr in range(2):
                            order = list(range(NT)) if dr == 0 else [1, 0] + list(range(17, 1, -1))
                            gcol = gC[pr0:pr0 + 64, dr * 8 + h:dr * 8 + h + 1]
                            if dr == 0:
                                S.op("act", lambda e, h=h, dr=dr: e.activation(
                                    out=kdh[dr][:], in_=Kr[:, :, h * 64:(h + 1) * 64], func=AF.Copy, scale=dtab[:, dr, h:h + 1]),
                                    reads=kr_all + [b_const], writes=[b_kdh[dr]])
                            else:
                                S.op("dve", lambda e, h=h, dr=dr: e.tensor_scalar(
                                    out=kdh[dr][:], in0=Kr[:, :, h * 64:(h + 1) * 64], scalar1=dtab[:, dr, h:h + 1], scalar2=None, op0=ALU.mult),
                                    reads=kr_all + [b_const], writes=[b_kdh[dr]])
                            prev = None
                            for step, tt in enumerate(order[:-1]):
                                ki = ucnt % 4
                                ui = ucnt % 8
                                ucnt += 1
                                uap = ps[ui % 2][pr0:pr0 + 64, (ui // 2) * 128:(ui // 2) * 128 + 128]
                                S.op("pe", lambda e, uap=uap, dr=dr, hp=hp, tt=tt: e.matmul(
                                    uap, lhsT=kdh[dr][:, tt, :], rhs=Vh[hp][:, tt, :], start=True, stop=True),
                                    reads=[b_kdh[dr], b_Vh[hp]], writes=[b_U[ui]])
                                if step == 0:
                                    dst = tmpS[pr0:pr0 + 64, 0, :]
                                    S.op("dve", lambda e, dst=dst, uap=uap: e.tensor_copy(out=dst, in_=uap),
                                         reads=[b_U[ui]], writes=[b_tmpS])
                                    prev = dst
                                else:
                                    if step == 1:
                                        c = 0 if dr == 0 else 15
                                    else:
                                        c = (step - 1) if dr == 0 else (15 - (step - 1))
                                    dst = Sst[dr][pr0:pr0 + 64, c, :]
                                    S.op("dve", lambda e, dst=dst, prev=prev, gcol=gcol, uap=uap: e.scalar_tensor_tensor(
                                        out=dst, in0=prev, scalar=gcol, in1=uap, op0=ALU.mult, op1=ALU.add),
                                        reads=[b_U[ui], b_tmpS, b_Sst[dr], b_const], writes=[b_Sst[dr]])
                                    prev = dst
                            S.op("act", lambda e, dr=dr, pr0=pr0: e.activation(out=Sbf[dr][pr0:pr0 + 64, :, :], in_=Sst[dr][pr0:pr0 + 64, :, :], func=AF.Copy),
                                 reads=[b_Sst[dr]], writes=[b_Sbf[dr]])
                        def phaseA(gp):
                            u = gp % 2
                            hb_in, hb_tr, hb_o, hb_cf, hb_cb = 2 * (2 + 3 * u), 2 * (2 + 3 * u) + 1, 2 * (3 + 3 * u), 2 * (4 + 3 * u), 2 * (4 + 3 * u) + 1
                            c0 = 2 * gp
                            for cc in range(2):
                                c = c0 + cc
                                S.op("pe", lambda e, hb_in=hb_in, cc=cc, c=c, pr0=pr0, pair=pair: e.matmul(
                                    hbank(hb_in)[:, cc * 128:(cc + 1) * 128], lhsT=KTr[pr0:pr0 + 64, pair, c * 128:(c + 1) * 128],
                                    rhs=QTr[pr0:pr0 + 64, pair, c * 128:(c + 1) * 128], start=True, stop=True),
                                    reads=[b_KT[c], b_QT[c]], writes=[hbk[hb_in]], signal=(cc == 1))
                            S.op("dve", lambda e, u=u, hb_in=hb_in, h=h: e.tensor_tensor(
                                out=PTt[u][:], in0=hbank(hb_in).rearrange("p (a t) -> p a t", a=2),
                                in1=retmask[:, h, :].unsqueeze(1).to_broadcast([128, 2, 128]), op=ALU.mult),
                                reads=[hbk[hb_in], b_const], writes=[b_PTt[u]])
                            for cc in range(2):
                                c = c0 + cc
                                S.op("pe", lambda e, hb_o=hb_o, cc=cc, c=c, u=u, hp=hp: e.matmul(
                                    hbank(hb_o)[:, cc * 128:(cc + 1) * 128], lhsT=PTt[u][:, cc, :], rhs=Vh[hp][:, c + 2, :], start=True, stop=True),
                                    reads=[b_PTt[u], b_Vh[hp]], writes=[hbk[hb_o]], signal=(cc == 1))
                            for dr, hbx in ((0, hb_cf), (1, hb_cb)):
                                for cc in range(2):
                                    c = c0 + cc
                                    S.op("pe", lambda e, hbx=hbx, cc=cc, c=c, dr=dr, pr0=pr0, pair=pair: e.matmul(
                                        hbank(hbx)[:, cc * 128:(cc + 1) * 128], lhsT=QTr[pr0:pr0 + 64, pair, c * 128:(c + 1) * 128],
                                        rhs=Sbf[dr][pr0:pr0 + 64, c, :], start=True, stop=True),
                                        reads=[b_QT[c], b_Sbf[dr]], writes=[hbk[hbx]], signal=(cc == 1))
                        def phaseB(gp):
                            u = gp % 2
                            hb_in, hb_tr, hb_o, hb_cf, hb_cb = 2 * (2 + 3 * u), 2 * (2 + 3 * u) + 1, 2 * (3 + 3 * u), 2 * (4 + 3 * u), 2 * (4 + 3 * u) + 1
                            c0 = 2 * gp
                            S.op("act", lambda e, u=u, hb_o=hb_o: e.activation(out=o1[u][:], in_=hbank(hb_o), func=AF.Copy),
                                 reads=[hbk[hb_o]], writes=[b_o[u]])
                            S.op("dve", lambda e, u=u, hb_cf=hb_cf, h=h: e.scalar_tensor_tensor(
                                out=o2[u][:], in0=hbank(hb_cf), scalar=dtab[:, 2, h:h + 1], in1=o1[u][:], op0=ALU.mult, op1=ALU.add),
                                reads=[hbk[hb_cf], b_o[u], b_const], writes=[b_o[u]])
                            S.op("dve", lambda e, u=u, hb_cb=hb_cb, h=h: e.scalar_tensor_tensor(
                                out=o3[u][:], in0=hbank(hb_cb), scalar=dtab[:, 3, h:h + 1], in1=o2[u][:], op0=ALU.mult, op1=ALU.add),
                                reads=[hbk[hb_cb], b_o[u], b_const], writes=[b_o[u]])
                            for cc in range(2):
                                S.op("dve", lambda e, u=u, cc=cc: e.bn_stats(out=bst[u][:, cc, :], in_=o3[u][:, cc * 128:(cc + 1) * 128]),
                                     reads=[b_o[u]], writes=[b_st[u]])
                                S.op("dve", lambda e, u=u, cc=cc: e.bn_aggr(out=mv[u][:, cc, :], in_=bst[u][:, cc, :]),
                                     reads=[b_st[u]], writes=[b_st[u]])
                            S.op("dve", lambda e, u=u: e.tensor_scalar(out=rs2[u][:, 0:2], in0=mv[u][:, :, 1], scalar1=EPS, scalar2=None, op0=ALU.add),
                                 reads=[b_st[u]], writes=[b_st[u]])
                            S.op("act", lambda e, u=u: e.activation(out=rs2[u][:, 0:2], in_=rs2[u][:, 0:2], func=AF.Sqrt),
                                 reads=[b_st[u]], writes=[b_st[u]])
                            S.op("dve", lambda e, u=u: e.reciprocal(out=rs2[u][:, 2:4], in_=rs2[u][:, 0:2]),
                                 reads=[b_st[u]], writes=[b_st[u]])
                            for cc in range(2):
                                S.op("dve", lambda e, u=u, cc=cc: e.tensor_scalar(
                                    out=nrm[u][:, cc * 128:(cc + 1) * 128], in0=o3[u][:, cc * 128:(cc + 1) * 128],
                                    scalar1=mv[u][:, cc, 0:1], scalar2=rs2[u][:, 2 + cc:3 + cc], op0=ALU.subtract, op1=ALU.mult),
                                    reads=[b_o[u], b_st[u]], writes=[b_nrm[u]])
                            S.op("dve", lambda e, u=u, hp=hp, c0=c0: e.tensor_tensor(
                                out=retm[u][:], in0=nrm[u][:].rearrange("p (a t) -> p a t", a=2), in1=Gh[hp][:, c0:c0 + 2, :], op=ALU.mult),
                                reads=[b_nrm[u], b_Gh[hp]], writes=[b_retm[u]])
                        def phaseC(gp):
                            u = gp % 2
                            hb_in, hb_tr, hb_o, hb_cf, hb_cb = 2 * (2 + 3 * u), 2 * (2 + 3 * u) + 1, 2 * (3 + 3 * u), 2 * (4 + 3 * u), 2 * (4 + 3 * u) + 1
                            c0 = 2 * gp
                            for cc in range(2):
                                S.op("pe", lambda e, u=u, cc=cc, hb_tr=hb_tr: e.transpose(
                                    out=hbank_bf(hb_tr)[:, cc * 128:(cc + 1) * 128], in_=retm[u][:, cc, :], identity=ident[:]),
                                    reads=[b_retm[u], b_const], writes=[hbk[hb_tr]], signal=(cc == 1))
                            S.op("act", lambda e, h=h, c0=c0, hb_tr=hb_tr: e.activation(
                                out=mixT[:, 8 + h, c0 * 128:(c0 + 2) * 128], in_=hbank_bf(hb_tr), func=AF.Copy),
                                reads=[hbk[hb_tr]], writes=[b_mix[(8 + h, c0)], b_mix[(8 + h, c0 + 1)]])
                            cast_step(1)
                        for g in range(10):
                            if g < 8:
                                phaseA(g)
                            if 1 <= g <= 8:
                                phaseB(g - 1)
                            if 2 <= g <= 9:
                                phaseC(g - 2)
                    cast_drain(True)
                    if debug and s == 0:
                        bd = S.buf("dbg")
                        S.dma("sp", lambda e: e.dma_start(out=mixdbg, in_=mixT[:]), sb=bd, reads=list(b_mix.values()))
                    S.end_stage()
                    chk("S2a" + ("" if "S2a" == "S0" else str(s)))

                with ExitStack() as st:
                    QTn = [sb(st, f"QTn{i}", [128, LAT], BF16) for i in range(2)]
                    KTn = [sb(st, f"KTn{i}", [128, LT], BF16) for i in range(2)]
                    Vn = [sb(st, f"Vn{i}", [128, NT, 132], BF16) for i in range(2)]
                    nabf = sb(st, "nabf", [128, 14, 256], F32)
                    EB = [sb(st, f"EB{i}", [128, 14, 256], BF16) for i in range(2)]
                    b_QTn = S.bufs(2, "QTn")
                    b_KTn = S.bufs(2, "KTn")
                    b_Vn = S.bufs(2, "Vn")
                    b_Vones = S.bufs(2, "Vones")
                    b_nabf = S.buf("nabf")
                    b_EB = S.bufs(2, "EB")
                    Ee = [sb(st, f"Ee{i}", [128, 256], BF16) for i in range(4)]
                    b_Ee = S.bufs(4, "Ee")
                    Pm = [sb(st, f"Pm{i}", [128, 8, 256], BF16) for i in range(2)]
                    b_Pm = [[S.buf("Pm") for _ in range(8)] for _ in range(2)]
                    on = [sb(st, f"on{i}", [128, 2, 128], BF16) for i in range(2)]
                    b_on = S.bufs(2, "on")
                    rcp = [sb(st, f"rcp{i}", [128, 2], F32) for i in range(2)]
                    b_rcp = S.bufs(2, "rcp")
                    sbk = [bankB[i % 3] for i in range(6)]
                    obk = [bankB[3 + i] for i in range(4)]
                    trb = bankB[7]
                    cast_step, cast_drain = make_caster(st, False)
                    for i in range(2):
                        S.op("dve", lambda e, i=i: e.memset(Vn[i][:, :, 128:132], 1.0), writes=[b_Vones[i]])
                    it = 0
                    ecnt = 0
                    for h in range(8):
                        hp = h % 2
                        S.dma("sp", lambda e, h=h, hp=hp: e.dma_start(out=QTn[hp][:], in_=QTs[h]), sb=b_QTn[hp], writes=[b_QTn[hp]])
                        S.dma("sp", lambda e, h=h, hp=hp: e.dma_start(out=KTn[hp][:], in_=KTs[h]), sb=b_KTn[hp], writes=[b_KTn[hp]])
                        S.dma("sp", lambda e, h=h, hp=hp: e.dma_start(
                            out=Vn[hp][:, :, 0:128], in_=PT[h // 4, :, (h % 4) * 128:(h % 4 + 1) * 128].rearrange("(t p) c -> p t c", p=128)),
                            sb=b_Vn[hp], writes=[b_Vn[hp]])
                        S.dma("sp", lambda e, h=h: e.dma_start(out=nabf[:], in_=nab_d[h]), sb=b_nabf, writes=[b_nabf])
                        S.op("act", lambda e, hp=hp: e.activation(out=EB[hp][:], in_=nabf[:], func=AF.Exp), reads=[b_nabf], writes=[b_EB[hp]])
                        for qt in range(8):
                            u = it % 2
                            it += 1
                            if qt == 0:
                                kcs = [0, 1, 2, 3]
                                tb = 6
                            elif qt == 7:
                                kcs = [12, 13, 14, 15]
                                tb = 10
                            else:
                                kcs = list(range(2 * qt - 2, 2 * qt + 4))
                                tb = 0
                            chunks = [(kc + 2, tb + i) for i, kc in enumerate(kcs)] + [(0, None), (1, None)]
                            for ci, (tt, tbi) in enumerate(chunks):
                                shb = ecnt % 6
                                ei = ecnt % 4
                                ecnt += 1
                                sap = ps[shb % 3][:, (shb // 3) * 256:(shb // 3) * 256 + 256]
                                S.op("pe", lambda e, sap=sap, hp=hp, tt=tt, qt=qt: e.matmul(
                                    sap, lhsT=KTn[hp][:, tt * 128:(tt + 1) * 128], rhs=QTn[hp][:, qt * 256:(qt + 1) * 256], start=True, stop=True),
                                    reads=[b_KTn[hp], b_QTn[hp]], writes=[sbk[shb]])
                                if tbi is None:
                                    S.op("act", lambda e, sap=sap, u=u, ci=ci: e.activation(out=Pm[u][:, ci, :], in_=sap, func=AF.Exp, scale=128.0 ** -0.5),
                                         reads=[sbk[shb]], writes=[b_Pm[u][ci]])
                                else:
                                    S.op("act", lambda e, sap=sap, ei=ei: e.activation(out=Ee[ei][:], in_=sap, func=AF.Exp, scale=128.0 ** -0.5),
                                         reads=[sbk[shb]], writes=[b_Ee[ei]])
                                    S.op("dve", lambda e, u=u, ci=ci, ei=ei, hp=hp, tbi=tbi: e.tensor_tensor(
                                        out=Pm[u][:, ci, :], in0=Ee[ei][:], in1=EB[hp][:, tbi, :], op=ALU.mult),
                                        reads=[b_Ee[ei], b_EB[hp]], writes=[b_Pm[u][ci]])
                            nck = len(chunks)
                            for qh in range(2):
                                ob = u * 2 + qh
                                for ci, (tt, tbi) in enumerate(chunks):
                                    S.op("pe", lambda e, ob=ob, u=u, ci=ci, qh=qh, hp=hp, tt=tt, nck=nck: e.matmul(
                                        ps[3 + ob][:, 0:129], lhsT=Pm[u][:, ci, qh * 128:(qh + 1) * 128], rhs=Vn[hp][:, tt, 0:129],
                                        start=(ci == 0), stop=(ci == nck - 1)),
                                        reads=[b_Pm[u][ci], b_Vn[hp], b_Vones[hp]], writes=[obk[ob]], signal=(ci == nck - 1))
                                S.op("dve", lambda e, ob=ob, u=u, qh=qh: e.reciprocal(out=rcp[u][:, qh:qh + 1], in_=ps[3 + ob][:, 128:129]),
                                     reads=[obk[ob]], writes=[b_rcp[u]])
                                S.op("act" if qh == 0 else "dve",
                                     (lambda e, ob=ob, u=u, qh=qh: e.activation(out=on[u][:, qh, :], in_=ps[3 + ob][:, 0:128], func=AF.Copy, scale=rcp[u][:, qh:qh + 1]))
                                     if qh == 0 else
                                     (lambda e, ob=ob, u=u, qh=qh: e.tensor_scalar(out=on[u][:, qh, :], in0=ps[3 + ob][:, 0:128], scalar1=rcp[u][:, qh:qh + 1], scalar2=None, op0=ALU.mult)),
                                     reads=[obk[ob], b_rcp[u]], writes=[b_on[u]])
                            for qh in range(2):
                                S.op("pe", lambda e, u=u, qh=qh: e.transpose(
                                    out=psb(7)[:, (u * 2 + qh) * 128:(u * 2 + qh + 1) * 128], in_=on[u][:, qh, :], identity=ident[:]),
                                    reads=[b_on[u], b_const], writes=[trb], signal=(qh == 1))
                            S.op("dve", lambda e, u=u, h=h, qt=qt: e.tensor_copy(
                                out=mixT[:, h, qt * 256:(qt + 1) * 256], in_=psb(7)[:, u * 256:(u + 1) * 256]),
                                reads=[trb], writes=[b_mix[(h, 2 * qt)], b_mix[(h, 2 * qt + 1)]])
                            cast_step(2)
                    cast_drain(True)
                    if debug and s == 0:
                        bd = S.buf("dbg")
                        S.dma("sp", lambda e: e.dma_start(out=mixdbg, in_=mixT[:]), sb=bd, reads=list(b_mix.values()))
                    S.end_stage()
                    chk("S2b" + ("" if "S2b" == "S0" else str(s)))


                with ExitStack() as st:
                    wo = sb(st, "wo", [128, KC, D], BF16)
                    b_wo = S.bufs(4, "wo")
                    G1 = sb(st, "G1", [128, D], F32)
                    b_G1 = S.buf("G1")
                    xin = [sb(st, f"x3_{i}", [128, D], F32) for i in range(2)]
                    b_x = S.bufs(2, "x3")
                    tt_ = [sb(st, f"t3_{i}", [128, D], F32) for i in range(2)]
                    b_t = S.bufs(2, "t3")
                    xo = [sb(st, f"xo_{i}", [128, D], F32) for i in range(2)]
                    b_xo = S.bufs(2, "xo")
                    junk = sb(st, "junk3", [128, 512], BF16)
                    b_junk = S.buf("junk3")
                    ssp = [sb(st, f"ssp{i}", [128, 8], F32) for i in range(2)]
                    bst3 = [sb(st, f"bst3{i}", [128, 4, 6], F32) for i in range(2)]
                    b_ss = S.bufs(2, "ss3")
                    b_pb = S.bufs(8, "pb3")
                    for q in range(4):
                        S.dma("sp", lambda e, q=q: e.dma_start(
                            out=wo[:, :, q * 512:(q + 1) * 512], in_=woutb[:, q * 512:(q + 1) * 512].rearrange("(j p) c -> p j c", p=128)),
                            sb=b_wo[q], reads=[wbufs["wout"]], writes=[b_wo[q]])
                    S.dma("sp", lambda e, s=s: e.dma_start(out=G1[:], in_=gvec[s, :].partition_broadcast(128)), sb=b_G1, writes=[b_G1])
                    for lt in range(16):
                        u = lt % 2
                        S.dma("sp", lambda e, lt=lt, u=u, s=s: e.dma_start(out=xin[u][:], in_=x2[s, lt * 128:(lt + 1) * 128, :]), sb=b_x[u], writes=[b_x[u]])
                        for q in range(4):
                            bk = u * 4 + q
                            for j in range(KC):
                                S.op("pe", lambda e, bk=bk, j=j, lt=lt, q=q: e.matmul(
                                    ps[bk][:], lhsT=mixT[:, j, lt * 128:(lt + 1) * 128], rhs=wo[:, j, q * 512:(q + 1) * 512],
                                    start=(j == 0), stop=(j == KC - 1)),
                                    reads=[b_mix[(j, lt)], b_wo[q]], writes=[b_pb[bk]], signal=(j == KC - 1))
                            S.op("dve", lambda e, bk=bk, u=u, q=q: e.bn_stats(out=bst3[u][:, q, :], in_=ps[bk][:]),
                                 reads=[b_pb[bk]], writes=[b_ss[u]])
                        S.op("dve", lambda e, u=u: e.bn_aggr(out=ssp[u][:, 0:2], in_=bst3[u][:].rearrange("p q s -> p (q s)")),
                             reads=[b_ss[u]], writes=[b_ss[u]])
                        S.op("dve", lambda e, u=u: e.scalar_tensor_tensor(out=ssp[u][:, 5:6], in0=ssp[u][:, 0:1], scalar=ssp[u][:, 0:1],
                                                                           in1=ssp[u][:, 1:2], op0=ALU.mult, op1=ALU.add),
                             reads=[b_ss[u]], writes=[b_ss[u]])
                        S.op("dve", lambda e, u=u: e.tensor_scalar(out=ssp[u][:, 5:6], in0=ssp[u][:, 5:6], scalar1=EPS, scalar2=None, op0=ALU.add),
                             reads=[b_ss[u]], writes=[b_ss[u]])
                        S.op("act", lambda e, u=u: e.activation(out=ssp[u][:, 5:6], in_=ssp[u][:, 5:6], func=AF.Sqrt),
                             reads=[b_ss[u]], writes=[b_ss[u]])
                        S.op("dve", lambda e, u=u: e.reciprocal(out=ssp[u][:, 6:7], in_=ssp[u][:, 5:6]),
                             reads=[b_ss[u]], writes=[b_ss[u]])
                        for q in range(4):
                            bk = u * 4 + q
                            S.op("dve", lambda e, bk=bk, u=u, q=q: e.scalar_tensor_tensor(
                                out=tt_[u][:, q * 512:(q + 1) * 512], in0=ps[bk][:], scalar=ssp[u][:, 6:7], in1=G1[:, q * 512:(q + 1) * 512],
                                op0=ALU.mult, op1=ALU.mult), reads=[b_pb[bk], b_ss[u], b_G1], writes=[b_t[u]])
                        S.op("dve", lambda e, u=u: e.tensor_tensor(out=xo[u][:], in0=tt_[u][:], in1=xin[u][:], op=ALU.add),
                             reads=[b_t[u], b_x[u]], writes=[b_xo[u]])
                        S.dma("sp", lambda e, lt=lt, u=u, s=s: e.dma_start(out=xnew[s, lt * 128:(lt + 1) * 128, :], in_=xo[u][:]), sb=b_xo[u], reads=[b_xo[u]])
                    S.end_stage()
                    chk("S3" + ("" if "S3" == "S0" else str(s)))

        with ExitStack() as st:
            h2T = sb(st, "h2T", [128, KC, 512], BF16)
            actT = sb(st, "actT", [128, FC, 512], BF16)
            ring = [sb(st, f"ring{i}", [128, 22 * 512], BF16) for i in range(4)]
            b_ring = S.bufs(4, "ring")
            G2 = sb(st, "G2", [128, D], F32)
            b_G2 = S.buf("G2")
            ybuf = [sb(st, f"yb{i}", [128, D], F32) for i in range(2)]
            b_yb = S.bufs(2, "yb")
            xn = [sb(st, f"xn{i}", [128, D], F32) for i in range(2)]
            b_xn = S.bufs(2, "xn")
            ssq = [sb(st, f"ssq{i}", [128, 8], F32) for i in range(2)]
            bst4 = [sb(st, f"bst4{i}", [128, 4, 6], F32) for i in range(2)]
            b_sq = S.bufs(2, "sq")
            sg = [sb(st, f"sg{i}", [128, 512], F32) for i in range(2)]
            b_sg = S.bufs(2, "sg")
            junk = sb(st, "junk4", [128, 512], BF16)
            b_junk = S.buf("junk4")
            b_act = S.bufs(FC, "act")
            b_pb = S.bufs(8, "pb4")
            rc = 0
            xcnt = 0
            hb2 = {(j, ti): S.buf("h2") for j in range(KC) for ti in range(4)}
            ntile4 = make_norm(st, "f", h2T, [0, 1, 2, 3], b_pb[0:4], xin_ext=(xn, b_xn))
            for ft in range(8):
                s = ft // 4
                t0 = (ft % 4) * 512
                if ft % 4 == 0:
                    S.dma("sp", lambda e, s=s: e.dma_start(out=G2[:], in_=gvec[2 + s, :].partition_broadcast(128)), sb=b_G2, writes=[b_G2])
                for ti in range(4):
                    ntile4(xnew[s, t0 + ti * 128:t0 + (ti + 1) * 128, :], 6 + s, 8 + s, ti * 128, [hb2[(j, ti)] for j in range(KC)])
                h2all = list(hb2.values())
                for cgp in range(FC // 4):
                    rg, ru = rc % 4, (rc + 1) % 4
                    rc += 2
                    for r, wsrc, nm in ((rg, wgb, "wg"), (ru, wub, "wu")):
                        S.dma("sp", lambda e, r=r, wsrc=wsrc, cgp=cgp: e.dma_start(
                            out=ring[r][:, 0:KC * 512].rearrange("p (j c) -> p j c", c=512),
                            in_=wsrc[:, cgp * 512:(cgp + 1) * 512].rearrange("(j p) c -> p j c", p=128)),
                            sb=b_ring[r], reads=[wbufs[nm]], writes=[b_ring[r]])
                    for cc in range(4):
                        c = cgp * 4 + cc
                        u = c % 2
                        bg, bu = u * 2, u * 2 + 1
                        for r, bk in ((rg, bg), (ru, bu)):
                            wv = ring[r][:, 0:KC * 512].rearrange("p (j c) -> p j c", c=512)
                            for j in range(KC):
                                S.op("pe", lambda e, wv=wv, bk=bk, j=j, cc=cc: e.matmul(
                                    ps[bk][:], lhsT=wv[:, j, cc * 128:(cc + 1) * 128], rhs=h2T[:, j, :], start=(j == 0), stop=(j == KC - 1)),
                                    reads=[b_ring[r]] + ([] if j else h2all), writes=[b_pb[bk]], signal=(j == KC - 1))
                        S.op("act", lambda e, u=u, bg=bg: e.activation(out=sg[u][:], in_=ps[bg][:], func=AF.Silu),
                             reads=[b_pb[bg]], writes=[b_sg[u]])
                        S.op("dve", lambda e, u=u, bu=bu, c=c: e.tensor_tensor(out=actT[:, c, :], in0=ps[bu][:], in1=sg[u][:], op=ALU.mult),
                             reads=[b_pb[bu], b_sg[u]], writes=[b_act[c]])
                for tp in range(2):
                    for dg in range(4):
                        for half in range(2):
                            r = rc % 4
                            rc += 1
                            S.dma("sp", lambda e, r=r, dg=dg, half=half: e.dma_start(
                                out=ring[r][:].rearrange("p (c n) -> p c n", n=512),
                                in_=wdb[half * 22 * 128:(half + 1) * 22 * 128, dg * 512:(dg + 1) * 512].rearrange("(c p) n -> p c n", p=128)),
                                sb=b_ring[r], reads=[wbufs["wd"]], writes=[b_ring[r]])
                            wv = ring[r][:].rearrange("p (c n) -> p c n", n=512)
                            for tl in range(2):
                                ti = tp * 2 + tl
                                bk = 4 + (dg % 2) * 2 + tl
                                for c22 in range(22):
                                    c = half * 22 + c22
                                    S.op("pe", lambda e, wv=wv, bk=bk, c=c, c22=c22, ti=ti: e.matmul(
                                        ps[bk][:], lhsT=actT[:, c, ti * 128:(ti + 1) * 128], rhs=wv[:, c22, :],
                                        start=(c == 0), stop=(c == FC - 1)),
                                        reads=[b_ring[r], b_act[c]], writes=[b_pb[bk]], signal=(c22 == 21))
                        for tl in range(2):
                            bk = 4 + (dg % 2) * 2 + tl
                            S.op("dve", lambda e, bk=bk, tl=tl, dg=dg: e.bn_stats(out=bst4[tl][:, dg, :], in_=ps[bk][:]),
                                 reads=[b_pb[bk]], writes=[b_sq[tl]])
                            S.op("dve", lambda e, bk=bk, tl=tl, dg=dg: e.tensor_tensor(
                                out=ybuf[tl][:, dg * 512:(dg + 1) * 512], in0=ps[bk][:], in1=G2[:, dg * 512:(dg + 1) * 512], op=ALU.mult),
                                reads=[b_pb[bk], b_G2], writes=[b_yb[tl]])
                    for tl in range(2):
                        ti = tp * 2 + tl
                        xi = xcnt % 2
                        xcnt += 1
                        S.dma("sp", lambda e, xi=xi, s=s, t0=t0, ti=ti: e.dma_start(out=xn[xi][:], in_=xnew[s, t0 + ti * 128:t0 + (ti + 1) * 128, :]),
                              sb=b_xn[xi], writes=[b_xn[xi]])
                        S.op("dve", lambda e, tl=tl: e.bn_aggr(out=ssq[tl][:, 0:2], in_=bst4[tl][:].rearrange("p q s -> p (q s)")),
                             reads=[b_sq[tl]], writes=[b_sq[tl]])
                        S.op("dve", lambda e, tl=tl: e.scalar_tensor_tensor(out=ssq[tl][:, 5:6], in0=ssq[tl][:, 0:1], scalar=ssq[tl][:, 0:1],
                                                                             in1=ssq[tl][:, 1:2], op0=ALU.mult, op1=ALU.add),
                             reads=[b_sq[tl]], writes=[b_sq[tl]])
                        S.op("dve", lambda e, tl=tl: e.tensor_scalar(out=ssq[tl][:, 5:6], in0=ssq[tl][:, 5:6], scalar1=EPS, scalar2=None, op0=ALU.add),
                             reads=[b_sq[tl]], writes=[b_sq[tl]])
                        S.op("act", lambda e, tl=tl: e.activation(out=ssq[tl][:, 5:6], in_=ssq[tl][:, 5:6], func=AF.Sqrt),
                             reads=[b_sq[tl]], writes=[b_sq[tl]])
                        S.op("dve", lambda e, tl=tl: e.reciprocal(out=ssq[tl][:, 6:7], in_=ssq[tl][:, 5:6]),
                             reads=[b_sq[tl]], writes=[b_sq[tl]])
                        S.op("dve", lambda e, tl=tl, xi=xi: e.scalar_tensor_tensor(
                            out=xn[xi][:], in0=ybuf[tl][:], scalar=ssq[tl][:, 6:7], in1=xn[xi][:], op0=ALU.mult, op1=ALU.add),
                            reads=[b_yb[tl], b_sq[tl], b_xn[xi]], writes=[b_xn[xi]])
                        S.dma("sp", lambda e, xi=xi, s=s, t0=t0, ti=ti: e.dma_start(out=out2[s, t0 + ti * 128:t0 + (ti + 1) * 128, :], in_=xn[xi][:]),
                              sb=b_xn[xi], reads=[b_xn[xi]])
            S.barrier(final=True)

        with nc.Block() as block:
            S.emit(block)
    return nc


def _consts():
    p = np.arange(128)
    i = p[None, :].astype(np.float32)
    j = p[:, None].astype(np.float32)
    rcon = np.zeros((128, 5, 128), np.float32)
    rcon[:, 0] = np.maximum(i - j, 0)
    rcon[:, 1] = np.maximum(j - i, 0)
    rcon[:, 2] = (i > j) * 0.125
    rcon[:, 3] = (j > i) * 0.125
    rcon[:, 4] = (i == j) * 0.25
    pidx = np.stack([127 - p, p, p + 1, 128 - p], axis=1).astype(np.float32)
    quarter = 16
    inv = (10000.0 ** (-np.arange(quarter, dtype=np.float32) / quarter)).astype(np.float32)
    t = (np.arange(16)[None, :] * 128 + p[:, None])
    row = (t // 64).astype(np.float32)[..., None] * inv
    col = (t % 64).astype(np.float32)[..., None] * inv
    cr, sr, cc, sc = np.cos(row), np.sin(row), np.cos(col), np.sin(col)
    ropeC = np.concatenate([cr, cr, cc, cc], axis=-1).astype(np.float32)
    ropeS = np.concatenate([-sr, sr, -sc, sc], axis=-1).astype(np.float32)
    ident = np.eye(128, dtype=np.float32)
    return rcon, pidx, ropeC, ropeS, ident


def _nab_index():
    key = np.arange(128)[:, None, None]
    tb = np.arange(14)[None, :, None]
    q = np.arange(256)[None, None, :]
    kr, kc = key // 64, key % 64
    qr, qc = q // 64, q % 64
    dr_int = -4 + 2 * tb + kr - qr
    dr_top = 2 * (tb - 6) + kr - qr
    dr_bot = -4 + 2 * (tb - 10) + kr - qr
    dr = np.where(tb < 6, dr_int, np.where(tb < 10, dr_top, dr_bot))
    vr = np.where(tb < 6, (dr >= -4) & (dr <= 3), True)
    cs = np.clip(qc - 8, 0, 48)
    vc = (kc >= cs) & (kc < cs + 16)
    dc = kc - qc
    valid = vr & vc & (np.abs(dr) <= 7)
    idx_r = np.clip(dr + 7, 0, 14)
    idx_c = np.clip(dc + 15, 0, 30)
    return np.broadcast_to(idx_r, (128, 14, 256)), np.broadcast_to(idx_c, (128, 14, 256)), np.broadcast_to(valid, (128, 14, 256))


_NC_CACHE = {}


def _prep_inputs(inputs):
    f = lambda a: np.ascontiguousarray(np.asarray(a, dtype=np.float32))
    x, c, ctx, c_ctx = f(inputs["x"]), f(inputs["c"]), f(inputs["ctx"]), f(inputs["c_ctx"])
    rcon, pidx, ropeC, ropeS, ident = _consts()
    ada_w = f(inputs["ada_w"][0])
    ada_bT = f(inputs["ada_b"][0].reshape(96, 128).T)
    g4 = np.stack([inputs["norm_pre_mix"][0], inputs["norm_post_mix"][0], inputs["norm_pre_ffn"][0], inputs["norm_post_ffn"][0]])
    gT = f(np.asarray(g4, np.float32).reshape(4, 16, 128).transpose(2, 0, 1))
    lg = np.concatenate([np.asarray(inputs["ret_log_gamma_fwd"][0]), np.asarray(inputs["ret_log_gamma_bwd"][0])]).astype(np.float32)
    lgr = f(np.broadcast_to(lg[None, :], (128, 16)))
    rpb = np.asarray(inputs["na_rpb"][0], np.float32)
    ir, ic, valid = _nab_index()
    nab = np.where(valid[None], rpb[:, ir, ic], np.float32(NEG)).astype(np.float32)
    shared = {
        "ada_w": ada_w, "ada_bT": ada_bT, "gT": gT,
        "ada_b3": f(np.broadcast_to(np.asarray(inputs["ada_b"][0], np.float32)[None, :], (3, 6 * D))),
        "w_in": f(inputs["w_in"][0]), "w_out": f(inputs["w_out"][0]), "w_gate": f(inputs["w_gate"][0]),
        "w_up": f(inputs["w_up"][0]), "w_down": f(inputs["w_down"][0]),
        "nab": f(nab), "lgr": lgr, "ropeC": ropeC, "ropeS": ropeS, "ident": ident, "rcon": rcon, "pidx": pidx,
    }
    in_maps = []
    for i in range(8):
        cc3 = np.stack([c[2 * i], c[2 * i + 1], c_ctx], axis=0)
        cT = f(cc3.reshape(3, 16, 128).transpose(2, 1, 0))
        m = dict(shared)
        m["x2"] = f(x[2 * i:2 * i + 2])
        m["ctx2"] = f(ctx[2 * i:2 * i + 2])
        m["cT"] = cT
        in_maps.append(m)
    return in_maps


def kernel(**inputs):
    in_maps = _prep_inputs(inputs)
    if "nc" not in _NC_CACHE:
        _NC_CACHE["nc"] = build()
    nc = _NC_CACHE["nc"]
    res = run_bass_kernel_spmd(nc, in_maps, core_ids=list(range(8)))
    out = np.concatenate([np.asarray(r["out2"], dtype=np.float32) for r in res.results], axis=0)
    return out
```

```python
import numpy as np
from contextlib import ExitStack
import concourse.bass as bass
import concourse.mybir as mybir
from concourse.bass_utils import run_bass_kernel_spmd

F32 = mybir.dt.float32
BF16 = mybir.dt.bfloat16
AF = mybir.ActivationFunctionType
ALU = mybir.AluOpType

D = 2048
KC = 16
LAT = 2048
CTX = 256
LT = LAT + CTX
NT = LT // 128
DFF = 5632
FC = DFF // 128
INW = 6144
EPS = 1e-6
NEG = -30000.0
CAST_MODE = 1
NORM_STEPS = 8


class Buf:
    __slots__ = ("name", "w", "r", "dsem", "excl")

    def __init__(self, name="", excl=False):
        self.excl = excl
        self.name = name
        self.w = None
        self.r = {}
        self.dsem = None


class _Eng:
    def __init__(self, name):
        self.name = name
        self.ops = []
        self.waited = {}
        self.own = set()
        self.sem = None
        self.cnt = 0
        self.pending = False


class Sched:
    LIMIT = 30000

    def __init__(self, nc, stack, nsem=90):
        self.nc = nc
        self.sems = [stack.enter_context(nc.semaphore(f"sm{i}")) for i in range(nsem)]
        self.eng_free = list(range(16))[::-1]
        self.free = list(range(16, nsem))[::-1]
        self.semtot = [0] * nsem
        self.dma_sems = set()
        self.nobar = set()
        self.E = {n: _Eng(n) for n in ("pe", "act", "dve", "pool", "sp")}
        self.stage_bufs = []
        self.dead = False

    def newsem(self):
        return self.free.pop()

    def buf(self, name=""):
        b = Buf(name)
        self.stage_bufs.append(b)
        return b

    def bufs(self, n, name=""):
        return [self.buf(f"{name}{i}") for i in range(n)]

    def _eng_token(self, E, signal):
        if E.sem is None or (signal and E.cnt >= self.LIMIT and not E.pending):
            E.sem = self.eng_free.pop()
            E.own.add(E.sem)
            E.cnt = 0
        tok = (E.sem, E.cnt + 1)
        if signal:
            E.cnt += 1
            E.pending = False
            self.semtot[E.sem] = E.cnt
        else:
            E.pending = True
        return tok

    @staticmethod
    def _deps(reads, writes):
        d = []
        for b in reads:
            if b.w is not None:
                d.append(b.w)
        for b in writes:
            if b.w is not None:
                d.append(b.w)
            d.extend(b.r.items())
        return d

    @staticmethod
    def _need(E, deps):
        out = {}
        for si, v in deps:
            if E.name == "pe" and si in E.own:
                continue
            if E.waited.get(si, 0) >= v:
                continue
            if out.get(si, 0) < v:
                out[si] = v
        E.waited.update(out)
        return list(out.items())

    @staticmethod
    def _mark(tok, reads, writes):
        si, v = tok
        for b in reads:
            if b.r.get(si, 0) < v:
                b.r[si] = v
        for b in writes:
            b.w = tok
            b.r = {}

    def op(self, eng, fn, reads=(), writes=(), signal=True):
        if self.dead:
            return
        if any(b.excl for b in reads):
            writes = list(writes) + [b for b in reads if b.excl]
            reads = [b for b in reads if not b.excl]
        E = self.E[eng]
        waits = self._need(E, self._deps(reads, writes))
        tok = self._eng_token(E, signal)
        E.ops.append((waits, fn, tok if signal else None, 1))
        self._mark(tok, reads, writes)

    def dma(self, eng, fn, sb=None, reads=(), writes=(), sem=None):
        if self.dead:
            return
        E = self.E[eng]
        waits = self._need(E, self._deps(reads, writes))
        if sem is None:
            if sb.dsem is None:
                sb.dsem = self.newsem()
            sem = sb.dsem
        self.semtot[sem] += 16
        tok = (sem, self.semtot[sem])
        self.dma_sems.add(sem)
        E.ops.append((waits, fn, tok, 16))
        self._mark(tok, reads, writes)

    def barrier(self, final=False):
        if self.dead:
            return
        toks = []
        for E in self.E.values():
            assert not E.pending, E.name
            for si in E.own:
                if self.semtot[si] > 0:
                    toks.append((si, self.semtot[si]))
        for si in self.dma_sems:
            if si in self.nobar and not final:
                continue
            toks.append((si, self.semtot[si]))
        for E in self.E.values():
            waits = self._need(E, toks)
            if waits:
                E.ops.append((waits, None, None, 0))

    def end_stage(self):
        self.barrier()
        for b in self.stage_bufs:
            if b.dsem is not None and b.dsem not in self.nobar:
                self.free.append(b.dsem)
                b.dsem = None
        self.stage_bufs = []

    def emit(self, block):
        sems = self.sems

        def rep(name, e):
            for waits, fn, tok, inc in self.E[name].ops:
                for si, v in waits:
                    e.wait_ge(sems[si], v)
                if fn is not None:
                    ins = fn(e)
                    if tok is not None:
                        ins.then_inc(sems[tok[0]], inc)

        @block.tensor
        def _(e):
            rep("pe", e)

        @block.scalar
        def _(e):
            rep("act", e)

        @block.vector
        def _(e):
            rep("dve", e)

        @block.gpsimd
        def _(e):
            rep("pool", e)

        @block.sync
        def _(e):
            rep("sp", e)


class _Stop(Exception):
    pass


def build(debug=False, stop=None):
    nc = bass.Bass("TRN2", target_bir_lowering=False)
    dk = "ExternalOutput" if debug else "Internal"

    def din(name, shape, dt=F32):
        return nc.dram_tensor(name, list(shape), dt, kind="ExternalInput").ap()

    x2 = din("x2", [2, LAT, D])
    ctx2 = din("ctx2", [2, CTX, D])
    cT_d = din("cT", [128, KC, 3])
    adaw = din("ada_w", [D, 6 * D])
    adab_d = din("ada_bT", [128, 96])
    adab3_d = din("ada_b3", [3, 6 * D])
    gT_d = din("gT", [128, 4, KC])
    w_in = din("w_in", [D, INW])
    w_out = din("w_out", [D, D])
    w_gate = din("w_gate", [D, DFF])
    w_up = din("w_up", [D, DFF])
    w_down = din("w_down", [DFF, D])
    nab_d = din("nab", [8, 128, 14, 256])
    lgr_d = din("lgr", [128, 16])
    ropeC_d = din("ropeC", [128, 16, 64])
    ropeS_d = din("ropeS", [128, 16, 64])
    ident_d = din("ident", [128, 128])
    rcon_d = din("rcon", [128, 5, 128])
    pidx_d = din("pidx", [128, 4])
    out2 = nc.dram_tensor("out2", [2, LAT, D], F32, kind="ExternalOutput").ap()

    winb = nc.dram_tensor("winb", [D, INW], BF16).ap()
    woutb = nc.dram_tensor("woutb", [D, D], BF16).ap()
    wgb = nc.dram_tensor("wgb", [D, DFF], BF16).ap()
    wub = nc.dram_tensor("wub", [D, DFF], BF16).ap()
    wdb = nc.dram_tensor("wdb", [DFF, D], BF16).ap()
    QTs = nc.dram_tensor("QTs", [8, 128, LAT], BF16, kind=dk).ap()
    KTs = nc.dram_tensor("KTs", [8, 128, LT], BF16, kind=dk).ap()
    PT = nc.dram_tensor("PT", [8, LT, 512], BF16, kind=dk).ap()
    xnew = nc.dram_tensor("xnew", [2, LAT, D], F32, kind=dk).ap()
    gvec = nc.dram_tensor("gvec", [4, D], F32).ap()
    mixdbg = nc.dram_tensor("mixdbg", [128, KC, LAT], BF16, kind=dk).ap() if debug else None
    statdbg = nc.dram_tensor("statdbg", [128, NT, 4], F32, kind=dk).ap() if debug else None
    hTdbg = nc.dram_tensor("hTdbg", [128, KC, LT], BF16, kind=dk).ap() if debug else None

    with ExitStack() as top:
        S = Sched(nc, top)

        _uid = [0]

        def sb(stack, name, shape, dt):
            _uid[0] += 1
            return stack.enter_context(nc.sbuf_tensor(f"{name}_{_uid[0]}", list(shape), dt))

        ident = sb(top, "identb", [128, 128], BF16)
        AB = sb(top, "AB", [128, 10, KC], F32)
        lgr = sb(top, "lgr", [128, 16], F32)
        dtab = sb(top, "dtab", [128, 4, 8], F32)
        gC = sb(top, "gC", [128, 16], F32)
        retmask = sb(top, "retmask", [128, 8, 128], BF16)
        neghalf = sb(top, "neghalf", [128, 4], F32)
        dmy = sb(top, "dmy", [128, 4], F32)
        b_const = Buf("const")
        bankB = [Buf(f"bank{i}", excl=True) for i in range(8)]

        ps = [top.enter_context(nc.psum_tensor(f"ps{i}", [128, 512], F32)) for i in range(8)]

        def act_accum(fn, reads, writes):
            S.op("act", fn, reads=reads, writes=writes, signal=False)
            S.op("act", lambda e: e.activation(out=dmy[:, 1:2], in_=dmy[:, 0:1], func=AF.Copy))

        def psb(i):
            return ps[i][:].bitcast(BF16)

        def chk(tag):
            if stop == tag and not S.dead:
                S.barrier(final=True)
                S.dead = True

        with ExitStack() as st:
            wbufs = {}
            for name, src, dst, rows, last in (("win", w_in, winb, D, 8192),):
                b = Buf(name)
                sem = S.newsem()
                S.nobar.add(sem)
                nblk = 8
                rb = rows // nblk
                for i in range(nblk):
                    S.dma(
                        "pool",
                        lambda e, src=src, dst=dst, i=i, rb=rb, last=last: e.dma_start(
                            out=dst[i * rb:(i + 1) * rb, :], in_=src[i * rb:(i + 1) * rb, :],
                            max_dma_last_dim=last),
                        sem=sem, writes=[b] if i == nblk - 1 else [],
                    )
                wbufs[name] = b
            for name in ("wout", "wg", "wu", "wd"):
                wbufs[name] = Buf(name)
            cast_jobs = []
            for src, dst, rows, cols in ((w_out, woutb, D, D), (w_gate, wgb, D, DFF), (w_up, wub, D, DFF), (w_down, wdb, DFF, D)):
                for r0 in range(0, rows, 128):
                    for c0 in range(0, cols, 2048):
                        cw = min(2048, cols - c0)
                        cast_jobs.append((src[r0:r0 + 128, c0:c0 + cw], dst[r0:r0 + 128, c0:c0 + cw], cw))

            identf = sb(st, "identf", [128, 128], F32)
            cT = sb(st, "cT", [128, KC, 3], F32)
            sT = sb(st, "sT", [128, KC, 3], F32)
            adab = sb(st, "adab", [128, 96], F32)
            gT = sb(st, "gT", [128, 4, KC], F32)
            modT = sb(st, "modT", [128, 96, 3], F32)
            gtmp = sb(st, "gtmp", [128, 4, KC], F32)
            rcon = sb(st, "rcon", [128, 5, 128], F32)
            pidx = sb(st, "pidx", [128, 4], F32)
            tmp4 = sb(st, "tmp4", [128, 4, 8], F32)
            e1 = sb(st, "e1", [128, 128], F32)
            e2 = sb(st, "e2", [128, 128], F32)
            m1 = sb(st, "m1", [128, 128], F32)
            aw = [sb(st, f"aw{i}", [128, KC, 512], F32) for i in range(2)]
            b_aw = S.bufs(2, "aw")
            b_ld = S.buf("ld0")
            b_ps0 = S.buf("ps0")
            b_mod = S.buf("mod")
            b_tmp = S.buf("tmp")

            for dst, src in ((identf, ident_d), (cT, cT_d), (adab, adab_d), (gT, gT_d), (lgr, lgr_d),
                             (rcon, rcon_d), (pidx, pidx_d)):
                bb = S.buf("l")
                S.dma("sp", lambda e, dst=dst, src=src: e.dma_start(out=dst[:], in_=src), sb=bb, writes=[bb])
            loads_done = list(S.stage_bufs[-7:])
            S.op("dve", lambda e: e.memset(neghalf[:], -0.5), writes=[b_const])
            S.op("dve", lambda e: e.memset(dmy[:], 0.0), writes=[b_const])
            S.op("act", lambda e: e.activation(out=ident[:], in_=identf[:], func=AF.Copy), reads=loads_done, writes=[b_const])
            S.op("act", lambda e: e.activation(out=sT[:], in_=cT[:], func=AF.Silu), reads=loads_done, writes=[b_tmp])

            mrow = sb(st, "mrow", [4, 6 * D], F32)
            adab3 = sb(st, "adab3", [4, 6 * D], F32)
            b_mrow = S.buf("mrow")
            b_ab3 = S.buf("ab3")
            b_pa = S.bufs(4, "pa")
            S.dma("sp", lambda e: e.dma_start(out=adab3[0:3, :], in_=adab3_d), sb=b_ab3, writes=[b_ab3])
            for g in range(24):
                t = g % 2
                bk = 1 + g % 4
                S.dma("sp", lambda e, g=g, t=t: e.dma_start(
                    out=aw[t][:], in_=adaw[:, g * 512:(g + 1) * 512].rearrange("(j p) c -> p j c", p=128)),
                    sb=b_aw[t], writes=[b_aw[t]])
                for j in range(KC):
                    S.op("pe", lambda e, t=t, j=j, bk=bk: e.matmul(
                        ps[bk][0:3, :], lhsT=sT[:, j, :], rhs=aw[t][:, j, :], start=(j == 0), stop=(j == KC - 1)),
                        reads=[b_aw[t], b_tmp], writes=[b_pa[bk - 1]], signal=(j == KC - 1))
                S.op("dve", lambda e, g=g, bk=bk: e.tensor_tensor(
                    out=mrow[0:3, g * 512:(g + 1) * 512], in0=ps[bk][0:3, :], in1=adab3[0:3, g * 512:(g + 1) * 512], op=ALU.add),
                    reads=[b_pa[bk - 1], b_ab3], writes=[b_mrow])
            for ch in range(96):
                S.op("pe", lambda e, ch=ch: e.transpose(
                    out=ps[0][:, ch * 3:ch * 3 + 3], in_=mrow[0:3, ch * 128:(ch + 1) * 128], identity=identf[0:3, 0:3]),
                    reads=[b_mrow] + loads_done, writes=[b_ps0], signal=(ch == 95))
            S.op("dve", lambda e: e.tensor_copy(out=modT[:].rearrange("p c b -> p (c b)"), in_=ps[0][:, 0:288]),
                 reads=[b_ps0], writes=[b_mod])
            for b in range(3):
                S.op("dve", lambda e, b=b: e.scalar_tensor_tensor(
                    out=AB[:, b, :], in0=modT[:, 16:32, b], scalar=1.0, in1=gT[:, 0, :], op0=ALU.add, op1=ALU.mult),
                    reads=[b_mod], writes=[b_const])
                S.op("dve", lambda e, b=b: e.tensor_copy(out=AB[:, 3 + b, :], in_=modT[:, 0:16, b]),
                     reads=[b_mod], writes=[b_const])
            for b in range(2):
                S.op("dve", lambda e, b=b: e.scalar_tensor_tensor(
                    out=AB[:, 6 + b, :], in0=modT[:, 64:80, b], scalar=1.0, in1=gT[:, 2, :], op0=ALU.add, op1=ALU.mult),
                    reads=[b_mod], writes=[b_const])
                S.op("dve", lambda e, b=b: e.tensor_copy(out=AB[:, 8 + b, :], in_=modT[:, 48:64, b]),
                     reads=[b_mod], writes=[b_const])
                S.op("dve", lambda e, b=b: e.tensor_tensor(out=gtmp[:, b, :], in0=modT[:, 32:48, b], in1=gT[:, 1, :], op=ALU.mult),
                     reads=[b_mod], writes=[b_tmp])
                S.op("dve", lambda e, b=b: e.tensor_tensor(out=gtmp[:, 2 + b, :], in0=modT[:, 80:96, b], in1=gT[:, 3, :], op=ALU.mult),
                     reads=[b_mod], writes=[b_tmp])
            b_gv = S.buf("gv")
            S.dma("sp", lambda e: e.dma_start(out=gvec.rearrange("g (j p) -> p g j", p=128), in_=gtmp[:],
                                              allow_slow_non_contiguous=True), sb=b_gv, reads=[b_tmp], writes=[b_gv])

            for k in range(4):
                off = 0 if k in (0, 2) else 8
                S.op("dve", lambda e, k=k, off=off: e.tensor_scalar(
                    out=tmp4[:, k, :], in0=lgr[:, off:off + 8], scalar1=pidx[:, k:k + 1], scalar2=None, op0=ALU.mult),
                    reads=loads_done, writes=[b_tmp])
            S.op("act", lambda e: e.activation(out=dtab[:], in_=tmp4[:], func=AF.Exp), reads=[b_tmp], writes=[b_const])
            S.op("dve", lambda e: e.tensor_scalar(out=dtab[:, 2:4, :], in0=dtab[:, 2:4, :], scalar1=0.125, scalar2=None, op0=ALU.mult),
                 reads=[b_const], writes=[b_const])
            S.op("act", lambda e: e.activation(out=gC[:], in_=lgr[:], func=AF.Exp, scale=128.0), reads=loads_done, writes=[b_const])
            for h in range(8):
                S.op("act", lambda e, h=h: e.activation(out=e1[:], in_=rcon[:, 0, :], func=AF.Exp, scale=lgr[:, h:h + 1]),
                     reads=loads_done, writes=[b_tmp])
                S.op("act", lambda e, h=h: e.activation(out=e2[:], in_=rcon[:, 1, :], func=AF.Exp, scale=lgr[:, 8 + h:9 + h]),
                     reads=loads_done, writes=[b_tmp])
                S.op("dve", lambda e: e.tensor_tensor(out=m1[:], in0=e1[:], in1=rcon[:, 2, :], op=ALU.mult), reads=[b_tmp], writes=[b_tmp])
                S.op("dve", lambda e: e.tensor_tensor(out=e2[:], in0=e2[:], in1=rcon[:, 3, :], op=ALU.mult), reads=[b_tmp], writes=[b_tmp])
                S.op("dve", lambda e: e.tensor_tensor(out=m1[:], in0=m1[:], in1=e2[:], op=ALU.add), reads=[b_tmp], writes=[b_tmp])
                S.op("dve", lambda e, h=h: e.tensor_tensor(out=retmask[:, h, :], in0=m1[:], in1=rcon[:, 4, :], op=ALU.add),
                     reads=[b_tmp], writes=[b_const, b_tmp])
            S.end_stage()
            chk("S0" + ("" if "S0" == "S0" else str(s)))

        def make_norm(st, tag, hT, banks, bank_bufs, xin_ext=None):
            if xin_ext is None:
                xin = [sb(st, f"{tag}xin{i}", [128, D], F32) for i in range(2)]
                b_x = S.bufs(2, "x")
            else:
                xin, b_x = xin_ext
            yb = [sb(st, f"{tag}y{i}", [128, D], BF16) for i in range(2)]
            stat = [sb(st, f"{tag}stat{i}", [128, 4], F32) for i in range(2)]
            bstn = [sb(st, f"{tag}bstn{i}", [128, 4, 6], F32) for i in range(2)]
            b_y = S.bufs(2, "y")
            b_s = S.bufs(2, "s")
            state = {"n": 0, "l": 0, "s": 0}

            def load(src):
                t = state["l"] % 2
                state["l"] += 1
                S.dma("sp", lambda e: e.dma_start(out=xin[t][:], in_=src), sb=b_x[t], writes=[b_x[t]])

            def stats():
                t = state["s"] % 2
                state["s"] += 1
                for q in range(4):
                    S.op("dve", lambda e, q=q: e.bn_stats(out=bstn[t][:, q, :], in_=xin[t][:, q * 512:(q + 1) * 512]),
                         reads=[b_x[t]], writes=[b_s[t]])
                S.op("dve", lambda e: e.bn_aggr(out=stat[t][:, 0:2], in_=bstn[t][:].rearrange("p q s -> p (q s)")),
                     reads=[b_s[t]], writes=[b_s[t]])
                S.op("dve", lambda e: e.scalar_tensor_tensor(out=stat[t][:, 2:3], in0=stat[t][:, 0:1], scalar=stat[t][:, 0:1],
                                                             in1=stat[t][:, 1:2], op0=ALU.mult, op1=ALU.add),
                     reads=[b_s[t]], writes=[b_s[t]])
                S.op("dve", lambda e: e.tensor_scalar(out=stat[t][:, 2:3], in0=stat[t][:, 2:3], scalar1=EPS, scalar2=None, op0=ALU.add),
                     reads=[b_s[t]], writes=[b_s[t]])
                S.op("act", lambda e: e.activation(out=stat[t][:, 2:3], in_=stat[t][:, 2:3], func=AF.Sqrt),
                     reads=[b_s[t]], writes=[b_s[t]])
                S.op("dve", lambda e: e.reciprocal(out=stat[t][:, 3:4], in_=stat[t][:, 2:3]),
                     reads=[b_s[t]], writes=[b_s[t]])
                S.op("act", lambda e: e.activation(out=yb[t][:], in_=xin[t][:], func=AF.Copy, scale=stat[t][:, 3:4]),
                     reads=[b_x[t], b_s[t]], writes=[b_y[t]])

            def xpose(ai, bi, c0, hbj):
                t = state["n"] % 2
                state["n"] += 1
                for half in range(2):
                    bk = banks[t * 2 + half]
                    bb = bank_bufs[t * 2 + half]
                    for jj in range(8):
                        j = half * 8 + jj
                        S.op("pe", lambda e, bk=bk, jj=jj, j=j: e.transpose(
                            out=psb(bk)[:, jj * 128:(jj + 1) * 128], in_=yb[t][:, j * 128:(j + 1) * 128], identity=ident[:]),
                            reads=[b_y[t], b_const], writes=[bb], signal=(jj == 7))
                    for jj in range(8):
                        j = half * 8 + jj
                        S.op("dve", lambda e, bk=bk, jj=jj, j=j: e.tensor_scalar(
                            out=hT[:, j, c0:c0 + 128], in0=psb(bk)[:, jj * 128:(jj + 1) * 128],
                            scalar1=AB[:, ai, j:j + 1], scalar2=AB[:, bi, j:j + 1], op0=ALU.mult, op1=ALU.add),
                            reads=[bb, b_const], writes=[hbj[j]])
            def tile(src, ai, bi, c0, hbj):
                if state["l"] == state["n"]:
                    load(src)
                if state["s"] == state["n"]:
                    stats()
                xpose(ai, bi, c0, hbj)

            tile.load = load
            tile.stats = stats
            tile.xpose = xpose
            return tile

        cpos = {"next": 0}

        def make_caster(stk, active, nbuf=2, both=False, limit=None):
            if not active:
                return (lambda n=1: None), (lambda all_jobs=False: None)
            cin = [sb(stk, f"cin{i}", [128, 2048], F32) for i in range(nbuf)]
            cout = [sb(stk, f"cout{i}", [128, 2048], BF16) for i in range(nbuf)]
            b_cin = S.bufs(nbuf, "cin")
            b_cout = S.bufs(nbuf, "cout")
            pend = []
            njobs = len(cast_jobs) if limit is None else limit

            def load():
                k = cpos["next"]
                if k >= njobs:
                    return
                cpos["next"] = k + 1
                srcap, dstap, cw = cast_jobs[k]
                u = k % nbuf
                S.dma("sp", lambda e: e.dma_start(out=cin[u][:, 0:cw], in_=srcap), sb=b_cin[u], writes=[b_cin[u]])
                pend.append((u, dstap, cw))

            def finish():
                if not pend:
                    return
                u, dstap, cw = pend.pop(0)
                if both and u % 2 == 1:
                    S.op("dve", lambda e: e.tensor_copy(out=cout[u][:, 0:cw], in_=cin[u][:, 0:cw]),
                         reads=[b_cin[u]], writes=[b_cout[u]])
                else:
                    S.op("act", lambda e: e.activation(out=cout[u][:, 0:cw], in_=cin[u][:, 0:cw], func=AF.Copy),
                         reads=[b_cin[u]], writes=[b_cout[u]])
                S.dma("sp", lambda e: e.dma_start(out=dstap, in_=cout[u][:, 0:cw]), sb=b_cout[u], reads=[b_cout[u]])

            def step(n=1):
                if CAST_MODE == 0:
                    return
                for _ in range(n):
                    if len(pend) >= nbuf or cpos["next"] >= njobs:
                        finish()
                    load()

            def drain(all_jobs=False):
                while pend:
                    finish()
                if all_jobs:
                    while cpos["next"] < njobs:
                        load()
                        finish()
            return step, drain

        for s in range(2):
            with ExitStack() as st:
                hT = sb(st, "hT", [128, KC, LT], BF16)
                hb = {(j, ti): S.buf("h") for j in range(KC) for ti in range(NT)}
                srcs = [ctx2[s, ti * 128:(ti + 1) * 128, :] for ti in range(2)] + \
                       [x2[s, ti * 128:(ti + 1) * 128, :] for ti in range(16)]
                a_idx = [2, 2] + [s] * 16
                b_idx = [5, 5] + [3 + s] * 16
                wb = [sb(st, f"wb{i}", [128, KC, 512], BF16) for i in range(2)]
                b_wb = S.bufs(2, "wb")
                stg = [sb(st, f"stg{i}", [128, 512], BF16) for i in range(4)]
                b_stg = S.bufs(4, "stg")
                b_pp = S.bufs(4, "pp")
                cnt = 0

                def load_wb(cg, t):
                    S.dma("sp", lambda e: e.dma_start(
                        out=wb[t][:], in_=winb[:, cg * 512:(cg + 1) * 512].rearrange("(j p) c -> p j c", p=128)),
                        sb=b_wb[t], reads=[wbufs["win"]], writes=[b_wb[t]])

                def proj_tm(cg, t, ti):
                    nonlocal cnt
                    is_gate = cg in (10, 11)
                    bk = 4 + cnt % 4
                    si = cnt % 4
                    cnt += 1
                    for j in range(KC):
                        S.op("pe", lambda e, j=j: e.matmul(
                            ps[bk][:], lhsT=hT[:, j, ti * 128:(ti + 1) * 128], rhs=wb[t][:, j, :],
                            start=(j == 0), stop=(j == KC - 1)),
                            reads=[b_wb[t], hb[(j, ti)]], writes=[b_pp[bk - 4]], signal=(j == KC - 1))
                    if is_gate:
                        S.op("act", lambda e: e.activation(out=stg[si][:], in_=ps[bk][:], func=AF.Silu),
                             reads=[b_pp[bk - 4]], writes=[b_stg[si]])
                    elif cnt % 2 == 0 or cg in (4, 5):
                        S.op("act", lambda e: e.activation(out=stg[si][:], in_=ps[bk][:], func=AF.Copy),
                             reads=[b_pp[bk - 4]], writes=[b_stg[si]])
                    else:
                        S.op("dve", lambda e: e.tensor_copy(out=stg[si][:], in_=ps[bk][:]),
                             reads=[b_pp[bk - 4]], writes=[b_stg[si]])
                    S.dma("sp", lambda e: e.dma_start(out=PT[cg - 4, ti * 128:(ti + 1) * 128, :], in_=stg[si][:]),
                          sb=b_stg[si], reads=[b_stg[si]])
                    if cg not in (4, 5):
                        cast1_step(2)

                ntile = make_norm(st, f"a{s}", hT, [0, 1, 2, 3], S.bufs(4, "pt"))
                cast1_step, cast1_drain = make_caster(st, s == 1, nbuf=4, both=True)
                load_wb(4, 0)
                load_wb(5, 1)
                ntile.load(srcs[0])
                ntile.load(srcs[1])
                ntile.stats()
                for ti in range(NT):
                    if ti + 1 < NT:
                        ntile.stats()
                    if ti + 2 < NT:
                        ntile.load(srcs[ti + 2])
                    ntile.xpose(a_idx[ti], b_idx[ti], ti * 128, [hb[(j, ti)] for j in range(KC)])
                    if ti >= 1:
                        proj_tm(4, 0, ti - 1)
                        proj_tm(5, 1, ti - 1)
                proj_tm(4, 0, NT - 1)
                proj_tm(5, 1, NT - 1)
                if debug and s == 0:
                    bd = S.buf("dbgh")
                    S.dma("sp", lambda e: e.dma_start(out=hTdbg, in_=hT[:]), sb=bd, reads=list(hb.values()))
                chk(f"S1n{s}")
                for k, cg in enumerate([0, 1, 2, 3, 6, 7, 8, 9, 10, 11]):
                    t = k % 2
                    if cg == 6:
                        chk(f"S1f{s}")
                    load_wb(cg, t)
                    if cg < 4:
                        isq = cg < 2
                        for hc in range(4):
                            head = (cg % 2) * 4 + hc
                            groups = [(256 + g * 512, 512) for g in range(4)]
                            if not isq:
                                groups = [(0, 256)] + groups
                            for (c0, n) in groups:
                                bk = 4 + cnt % 4
                                si = cnt % 4
                                cnt += 1
                                tiles = range(c0 // 128, (c0 + n) // 128)
                                for j in range(KC):
                                    S.op("pe", lambda e, bk=bk, t=t, j=j, hc=hc, c0=c0, n=n: e.matmul(
                                        ps[bk][:, 0:n], lhsT=wb[t][:, j, hc * 128:(hc + 1) * 128], rhs=hT[:, j, c0:c0 + n],
                                        start=(j == 0), stop=(j == KC - 1)),
                                        reads=[b_wb[t]] + [hb[(j, ti)] for ti in tiles], writes=[b_pp[bk - 4]],
                                        signal=(j == KC - 1))
                                if cnt % 2 == 0:
                                    S.op("act", lambda e, bk=bk, si=si, n=n: e.activation(out=stg[si][:, 0:n], in_=ps[bk][:, 0:n], func=AF.Copy),
                                         reads=[b_pp[bk - 4]], writes=[b_stg[si]])
                                else:
                                    S.op("dve", lambda e, bk=bk, si=si, n=n: e.tensor_copy(out=stg[si][:, 0:n], in_=ps[bk][:, 0:n]),
                                         reads=[b_pp[bk - 4]], writes=[b_stg[si]])
                                if isq:
                                    dst = QTs[head, :, c0 - 256:c0 - 256 + n]
                                else:
                                    dst = KTs[head, :, c0:c0 + n]
                                S.dma("sp", lambda e, dst=dst, si=si, n=n: e.dma_start(out=dst, in_=stg[si][:, 0:n]),
                                      sb=b_stg[si], reads=[b_stg[si]])
                    else:
                        need_ctx = cg in (4, 5, 7, 8, 9)
                        for ti in range(0 if need_ctx else 2, NT):
                            proj_tm(cg, t, ti)
                cast1_drain(True)
                S.barrier(final=True)
                S.end_stage()
                chk("S1" + ("" if "S1" == "S0" else str(s)))

            with ExitStack() as stm:
                mixT = sb(stm, "mixT", [128, KC, LAT], BF16)
                b_mix = {(j, c): Buf("mix") for j in range(KC) for c in range(16)}

                with ExitStack() as st:
                    Kr = sb(st, "Kr", [128, NT, 512], BF16)
                    QTr = sb(st, "QTr", [128, 4, LAT], BF16)
                    KTr = sb(st, "KTr", [128, 4, LAT], BF16)
                    b_Kr = S.buf("Kr")
                    b_QT = S.bufs(16, "QT")
                    b_KT = S.bufs(16, "KT")
                    hbk = [bankB[i // 2] for i in range(16)]

                    def hbank(i, n=256):
                        return ps[i // 2][:, (i % 2) * 256:(i % 2) * 256 + n]

                    def hbank_bf(i, n=256):
                        return psb(i // 2)[:, (i % 2) * 512:(i % 2) * 512 + n]

                    with ExitStack() as st2:
                        Qr = sb(st2, "Qr", [128, 16, 512], BF16)
                        rC = sb(st2, "rC", [128, 16, 64], F32)
                        rS = sb(st2, "rS", [128, 16, 64], F32)
                        t1 = [sb(st2, f"t1_{i}", [128, 512], F32) for i in range(2)]
                        t2 = [sb(st2, f"t2_{i}", [128, 512], F32) for i in range(2)]
                        b_Qr = S.buf("Qr")
                        b_rt = S.buf("rt")
                        b_t1 = S.bufs(2, "t1")
                        b_t2 = S.bufs(2, "t2")
                        b_Xt = {}
                        S.dma("sp", lambda e: e.dma_start(out=Qr[:], in_=PT[2, 256:, :].rearrange("(t p) c -> p t c", p=128)),
                              sb=b_Qr, writes=[b_Qr])
                        S.dma("sp", lambda e: e.dma_start(out=Kr[:], in_=PT[3, :, :].rearrange("(t p) c -> p t c", p=128)),
                              sb=b_Kr, writes=[b_Kr])
                        b_rc = S.buf("rc")
                        S.dma("sp", lambda e: e.dma_start(out=rC[:], in_=ropeC_d), sb=b_rc, writes=[b_rc])
                        b_rs = S.buf("rs")
                        S.dma("sp", lambda e: e.dma_start(out=rS[:], in_=ropeS_d), sb=b_rs, writes=[b_rs])
                        k = 0
                        for lt in range(16):
                            for which in range(2):
                                X = Qr[:, lt, :] if which == 0 else Kr[:, 2 + lt, :]
                                bX = S.buf("X")
                                b_Xt[(which, lt)] = bX
                                base = b_Qr if which == 0 else b_Kr
                                u = k % 2
                                k += 1
                                xv = X.rearrange("p (h d) -> p h d", h=8)
                                xs = X.rearrange("p (h a b c) -> p h a b c", h=8, a=2, b=2)
                                Cb = rC[:, lt, :].unsqueeze(1).to_broadcast([128, 8, 64])
                                Sb5 = rS[:, lt, :].rearrange("p (a b c) -> p a b c", a=2, b=2).unsqueeze(1).to_broadcast([128, 8, 2, 2, 16])
                                t1v = t1[u][:].rearrange("p (h d) -> p h d", h=8)
                                t2v = t2[u][:].rearrange("p (h a b c) -> p h a b c", h=8, a=2, b=2)
                                S.op("dve", lambda e, t1v=t1v, xv=xv, Cb=Cb: e.tensor_tensor(out=t1v, in0=xv, in1=Cb, op=ALU.mult),
                                     reads=[base, b_rc], writes=[b_t1[u]])
                                S.op("dve", lambda e, t2v=t2v, xs=xs, Sb5=Sb5: e.tensor_tensor(
                                    out=t2v[:, :, :, 0, :], in0=xs[:, :, :, 1, :], in1=Sb5[:, :, :, 0, :], op=ALU.mult),
                                    reads=[base, b_rs], writes=[b_t2[u]])
                                S.op("dve", lambda e, t2v=t2v, xs=xs, Sb5=Sb5: e.tensor_tensor(
                                    out=t2v[:, :, :, 1, :], in0=xs[:, :, :, 0, :], in1=Sb5[:, :, :, 1, :], op=ALU.mult),
                                    reads=[base, b_rs], writes=[b_t2[u]])
                                S.op("dve", lambda e, X=X, u=u: e.tensor_tensor(out=X, in0=t1[u][:], in1=t2[u][:], op=ALU.add),
                                     reads=[b_t1[u], b_t2[u], base], writes=[bX])
                                hbi = (k % 4)
                                for pr in range(4):
                                    S.op("pe", lambda e, X=X, pr=pr, hbi=hbi: e.transpose(
                                        out=hbank_bf(hbi, 512)[:, pr * 128:(pr + 1) * 128], in_=X[:, pr * 128:(pr + 1) * 128], identity=ident[:]),
                                        reads=[bX, b_const], writes=[hbk[hbi]], signal=(pr == 3))
                                dstT = QTr if which == 0 else KTr
                                bT = b_QT[lt] if which == 0 else b_KT[lt]
                                S.op("act" if which == 0 else "dve",
                                     (lambda e, dstT=dstT, lt=lt, hbi=hbi: e.activation(
                                         out=dstT[:, :, lt * 128:(lt + 1) * 128], in_=hbank_bf(hbi, 512).rearrange("p (a t) -> p a t", a=4), func=AF.Copy))
                                     if which == 0 else
                                     (lambda e, dstT=dstT, lt=lt, hbi=hbi: e.tensor_copy(
                                         out=dstT[:, :, lt * 128:(lt + 1) * 128], in_=hbank_bf(hbi, 512).rearrange("p (a t) -> p a t", a=4))),
                                     reads=[hbk[hbi]], writes=[bT])
                        kr_all = [b_Kr] + [b_Xt[(1, lt)] for lt in range(16)]
                        S.barrier()
                    Vh = [sb(st, f"Vh{i}", [128, NT, 128], BF16) for i in range(2)]
                    Gh = [sb(st, f"Gh{i}", [128, 16, 128], BF16) for i in range(2)]
                    b_Vh = S.bufs(2, "Vh")
                    b_Gh = S.bufs(2, "Gh")
                    Sst = [sb(st, f"Sst{i}", [128, 16, 128], F32) for i in range(2)]
                    Sbf = [sb(st, f"Sbf{i}", [128, 16, 128], BF16) for i in range(2)]
                    tmpS = sb(st, "tmpS", [128, 2, 128], F32)
                    b_Sst = S.bufs(2, "Sst")
                    b_Sbf = S.bufs(2, "Sbf")
                    b_tmpS = S.buf("tmpS")
                    kdh = [sb(st, f"kdh{i}", [128, NT, 64], BF16) for i in range(2)]
                    b_kdh = S.bufs(2, "kdh")
                    PTt = [sb(st, f"PTt{i}", [128, 2, 128], BF16) for i in range(2)]
                    b_PTt = S.bufs(2, "PTt")
                    o1 = [sb(st, f"o1_{i}", [128, 256], F32) for i in range(2)]
                    o2 = [sb(st, f"o2_{i}", [128, 256], F32) for i in range(2)]
                    o3 = [sb(st, f"o3_{i}", [128, 256], F32) for i in range(2)]
                    nrm = [sb(st, f"nrm{i}", [128, 256], F32) for i in range(2)]
                    retm = [sb(st, f"retm{i}", [128, 2, 128], BF16) for i in range(2)]
                    bst = [sb(st, f"bst{i}", [128, 2, 6], F32) for i in range(2)]
                    mv = [sb(st, f"mv{i}", [128, 2, 2], F32) for i in range(2)]
                    rs2 = [sb(st, f"rs2{i}", [128, 4], F32) for i in range(2)]
                    b_o = S.bufs(2, "o")
                    b_st = S.bufs(2, "st")
                    b_nrm = S.bufs(2, "nrm")
                    b_retm = S.bufs(2, "retm")
                    ucnt = 0
                    b_U = [bankB[i % 2] for i in range(8)]
                    cast_step, cast_drain = make_caster(st, s == 0, limit=16)
                    for h in range(8):
                        hp = h % 2
                        pr0 = hp * 64
                        pair = h // 2
                        S.dma("sp", lambda e, h=h, hp=hp: e.dma_start(
                            out=Vh[hp][:], in_=PT[4 + h // 4, :, (h % 4) * 128:(h % 4 + 1) * 128].rearrange("(t p) c -> p t c", p=128)),
                            sb=b_Vh[hp], writes=[b_Vh[hp]])
                        S.dma("sp", lambda e, h=h, hp=hp: e.dma_start(
                            out=Gh[hp][:], in_=PT[6 + h // 4, 256:, (h % 4) * 128:(h % 4 + 1) * 128].rearrange("(t p) c -> p t c", p=128)),
                            sb=b_Gh[hp], writes=[b_Gh[hp]])
                        for dr in range(2):
                            order = list(range(NT)) if dr == 0 else [1, 0] + list(range(17, 1, -1))
                            gcol = gC[pr0:pr0 + 64, dr * 8 + h:dr * 8 + h + 1]
                            if dr == 0:
                                S.op("act", lambda e, h=h, dr=dr: e.activation(
                                    out=kdh[dr][:], in_=Kr[:, :, h * 64:(h + 1) * 64], func=AF.Copy, scale=dtab[:, dr, h:h + 1]),
                                    reads=kr_all + [b_const], writes=[b_kdh[dr]])
                            else:
                                S.op("dve", lambda e, h=h, dr=dr: e.tensor_scalar(
                                    out=kdh[dr][:], in0=Kr[:, :, h * 64:(h + 1) * 64], scalar1=dtab[:, dr, h:h + 1], scalar2=None, op0=ALU.mult),
                                    reads=kr_all + [b_const], writes=[b_kdh[dr]])
                            prev = None
                            for step, tt in enumerate(order[:-1]):
                                ki = ucnt % 4
                                ui = ucnt % 8
                                ucnt += 1
                                uap = ps[ui % 2][pr0:pr0 + 64, (ui // 2) * 128:(ui // 2) * 128 + 128]
                                S.op("pe", lambda e, uap=uap, dr=dr, hp=hp, tt=tt: e.matmul(
                                    uap, lhsT=kdh[dr][:, tt, :], rhs=Vh[hp][:, tt, :], start=True, stop=True),
                                    reads=[b_kdh[dr], b_Vh[hp]], writes=[b_U[ui]])
                                if step == 0:
                                    dst = tmpS[pr0:pr0 + 64, 0, :]
                                    S.op("dve", lambda e, dst=dst, uap=uap: e.tensor_copy(out=dst, in_=uap),
                                         reads=[b_U[ui]], writes=[b_tmpS])
                                    prev = dst
                                else:
                                    if step == 1:
                                        c = 0 if dr == 0 else 15
                                    else:
                                        c = (step - 1) if dr == 0 else (15 - (step - 1))
                                    dst = Sst[dr][pr0:pr0 + 64, c, :]
                                    S.op("dve", lambda e, dst=dst, prev=prev, gcol=gcol, uap=uap: e.scalar_tensor_tensor(
                                        out=dst, in0=prev, scalar=gcol, in1=uap, op0=ALU.mult, op1=ALU.add),
                                        reads=[b_U[ui], b_tmpS, b_Sst[dr], b_const], writes=[b_Sst[dr]])
                                    prev = dst
                            S.op("act", lambda e, dr=dr, pr0=pr0: e.activation(out=Sbf[dr][pr0:pr0 + 64, :, :], in_=Sst[dr][pr0:pr0 + 64, :, :], func=AF.Copy),
                                 reads=[b_Sst[dr]], writes=[b_Sbf[dr]])
                        def phaseA(gp):
                            u = gp % 2
                            hb_in, hb_tr, hb_o, hb_cf, hb_cb = 2 * (2 + 3 * u), 2 * (2 + 3 * u) + 1, 2 * (3 + 3 * u), 2 * (4 + 3 * u), 2 * (4 + 3 * u) + 1
                            c0 = 2 * gp
                            for cc in range(2):
                                c = c0 + cc
                                S.op("pe", lambda e, hb_in=hb_in, cc=cc, c=c, pr0=pr0, pair=pair: e.matmul(
                                    hbank(hb_in)[:, cc * 128:(cc + 1) * 128], lhsT=KTr[pr0:pr0 + 64, pair, c * 128:(c + 1) * 128],
                                    rhs=QTr[pr0:pr0 + 64, pair, c * 128:(c + 1) * 128], start=True, stop=True),
                                    reads=[b_KT[c], b_QT[c]], writes=[hbk[hb_in]], signal=(cc == 1))
                            S.op("dve", lambda e, u=u, hb_in=hb_in, h=h: e.tensor_tensor(
                                out=PTt[u][:], in0=hbank(hb_in).rearrange("p (a t) -> p a t", a=2),
                                in1=retmask[:, h, :].unsqueeze(1).to_broadcast([128, 2, 128]), op=ALU.mult),
                                reads=[hbk[hb_in], b_const], writes=[b_PTt[u]])
                            for cc in range(2):
                                c = c0 + cc
                                S.op("pe", lambda e, hb_o=hb_o, cc=cc, c=c, u=u, hp=hp: e.matmul(
                                    hbank(hb_o)[:, cc * 128:(cc + 1) * 128], lhsT=PTt[u][:, cc, :], rhs=Vh[hp][:, c + 2, :], start=True, stop=True),
                                    reads=[b_PTt[u], b_Vh[hp]], writes=[hbk[hb_o]], signal=(cc == 1))
                            for dr, hbx in ((0, hb_cf), (1, hb_cb)):
                                for cc in range(2):
                                    c = c0 + cc
                                    S.op("pe", lambda e, hbx=hbx, cc=cc, c=c, dr=dr, pr0=pr0, pair=pair: e.matmul(
                                        hbank(hbx)[:, cc * 128:(cc + 1) * 128], lhsT=QTr[pr0:pr0 + 64, pair, c * 128:(c + 1) * 128],
                                        rhs=Sbf[dr][pr0:pr0 + 64, c, :], start=True, stop=True),
                                        reads=[b_QT[c], b_Sbf[dr]], writes=[hbk[hbx]], signal=(cc == 1))
                        def phaseB(gp):
                            u = gp % 2
                            hb_in, hb_tr, hb_o, hb_cf, hb_cb = 2 * (2 + 3 * u), 2 * (2 + 3 * u) + 1, 2 * (3 + 3 * u), 2 * (4 + 3 * u), 2 * (4 + 3 * u) + 1
                            c0 = 2 * gp
                            S.op("act", lambda e, u=u, hb_o=hb_o: e.activation(out=o1[u][:], in_=hbank(hb_o), func=AF.Copy),
                                 reads=[hbk[hb_o]], writes=[b_o[u]])
                            S.op("dve", lambda e, u=u, hb_cf=hb_cf, h=h: e.scalar_tensor_tensor(
                                out=o2[u][:], in0=hbank(hb_cf), scalar=dtab[:, 2, h:h + 1], in1=o1[u][:], op0=ALU.mult, op1=ALU.add),
                                reads=[hbk[hb_cf], b_o[u], b_const], writes=[b_o[u]])
                            S.op("dve", lambda e, u=u, hb_cb=hb_cb, h=h: e.scalar_tensor_tensor(
                                out=o3[u][:], in0=hbank(hb_cb), scalar=dtab[:, 3, h:h + 1], in1=o2[u][:], op0=ALU.mult, op1=ALU.add),
                                reads=[hbk[hb_cb], b_o[u], b_const], writes=[b_o[u]])
                            for cc in range(2):
                                S.op("dve", lambda e, u=u, cc=cc: e.bn_stats(out=bst[u][:, cc, :], in_=o3[u][:, cc * 128:(cc + 1) * 128]),
                                     reads=[b_o[u]], writes=[b_st[u]])
                                S.op("dve", lambda e, u=u, cc=cc: e.bn_aggr(out=mv[u][:, cc, :], in_=bst[u][:, cc, :]),
                                     reads=[b_st[u]], writes=[b_st[u]])
                            S.op("dve", lambda e, u=u: e.tensor_scalar(out=rs2[u][:, 0:2], in0=mv[u][:, :, 1], scalar1=EPS, scalar2=None, op0=ALU.add),
                                 reads=[b_st[u]], writes=[b_st[u]])
                            S.op("act", lambda e, u=u: e.activation(out=rs2[u][:, 0:2], in_=rs2[u][:, 0:2], func=AF.Sqrt),
                                 reads=[b_st[u]], writes=[b_st[u]])
                            S.op("dve", lambda e, u=u: e.reciprocal(out=rs2[u][:, 2:4], in_=rs2[u][:, 0:2]),
                                 reads=[b_st[u]], writes=[b_st[u]])
                            for cc in range(2):
                                S.op("dve", lambda e, u=u, cc=cc: e.tensor_scalar(
                                    out=nrm[u][:, cc * 128:(cc + 1) * 128], in0=o3[u][:, cc * 128:(cc + 1) * 128],
                                    scalar1=mv[u][:, cc, 0:1], scalar2=rs2[u][:, 2 + cc:3 + cc], op0=ALU.subtract, op1=ALU.mult),
                                    reads=[b_o[u], b_st[u]], writes=[b_nrm[u]])
                            S.op("dve", lambda e, u=u, hp=hp, c0=c0: e.tensor_tensor(
                                out=retm[u][:], in0=nrm[u][:].rearrange("p (a t) -> p a t", a=2), in1=Gh[hp][:, c0:c0 + 2, :], op=ALU.mult),
                                reads=[b_nrm[u], b_Gh[hp]], writes=[b_retm[u]])
                        def phaseC(gp):
                            u = gp % 2
                            hb_in, hb_tr, hb_o, hb_cf, hb_cb = 2 * (2 + 3 * u), 2 * (2 + 3 * u) + 1, 2 * (3 + 3 * u), 2 * (4 + 3 * u), 2 * (4 + 3 * u) + 1
                            c0 = 2 * gp
                            for cc in range(2):
                                S.op("pe", lambda e, u=u, cc=cc, hb_tr=hb_tr: e.transpose(
                                    out=hbank_bf(hb_tr)[:, cc * 128:(cc + 1) * 128], in_=retm[u][:, cc, :], identity=ident[:]),
                                    reads=[b_retm[u], b_const], writes=[hbk[hb_tr]], signal=(cc == 1))
                            S.op("act", lambda e, h=h, c0=c0, hb_tr=hb_tr: e.activation(
                                out=mixT[:, 8 + h, c0 * 128:(c0 + 2) * 128], in_=hbank_bf(hb_tr), func=AF.Copy),
                                reads=[hbk[hb_tr]], writes=[b_mix[(8 + h, c0)], b_mix[(8 + h, c0 + 1)]])
                            cast_step(1)
                        for g in range(10):
                            if g < 8:
                                phaseA(g)
                            if 1 <= g <= 8:
                                phaseB(g - 1)
                            if 2 <= g <= 9:
                                phaseC(g - 2)
                    cast_drain(True)
                    if debug and s == 0:
                        bd = S.buf("dbg")
                        S.dma("sp", lambda e: e.dma_start(out=mixdbg, in_=mixT[:]), sb=bd, reads=list(b_mix.values()))
                    S.end_stage()
                    chk("S2a" + ("" if "S2a" == "S0" else str(s)))

                with ExitStack() as st:
                    QTn = [sb(st, f"QTn{i}", [128, LAT], BF16) for i in range(2)]
                    KTn = [sb(st, f"KTn{i}", [128, LT], BF16) for i in range(2)]
                    Vn = [sb(st, f"Vn{i}", [128, NT, 132], BF16) for i in range(2)]
                    nabf = sb(st, "nabf", [128, 14, 256], F32)
                    EB = [sb(st, f"EB{i}", [128, 14, 256], BF16) for i in range(2)]
                    b_QTn = S.bufs(2, "QTn")
                    b_KTn = S.bufs(2, "KTn")
                    b_Vn = S.bufs(2, "Vn")
                    b_Vones = S.bufs(2, "Vones")
                    b_nabf = S.buf("nabf")
                    b_EB = S.bufs(2, "EB")
                    Ee = [sb(st, f"Ee{i}", [128, 256], BF16) for i in range(4)]
                    b_Ee = S.bufs(4, "Ee")
                    Pm = [sb(st, f"Pm{i}", [128, 8, 256], BF16) for i in range(2)]
                    b_Pm = [[S.buf("Pm") for _ in range(8)] for _ in range(2)]
                    on = [sb(st, f"on{i}", [128, 2, 128], BF16) for i in range(2)]
                    b_on = S.bufs(2, "on")
                    rcp = [sb(st, f"rcp{i}", [128, 2], F32) for i in range(2)]
                    b_rcp = S.bufs(2, "rcp")
                    sbk = [bankB[i % 3] for i in range(6)]
                    obk = [bankB[3 + i] for i in range(4)]
                    trb = bankB[7]
                    cast_step, cast_drain = make_caster(st, False)
                    for i in range(2):
                        S.op("dve", lambda e, i=i: e.memset(Vn[i][:, :, 128:132], 1.0), writes=[b_Vones[i]])
                    it = 0
                    ecnt = 0
                    for h in range(8):
                        hp = h % 2
                        S.dma("sp", lambda e, h=h, hp=hp: e.dma_start(out=QTn[hp][:], in_=QTs[h]), sb=b_QTn[hp], writes=[b_QTn[hp]])
                        S.dma("sp", lambda e, h=h, hp=hp: e.dma_start(out=KTn[hp][:], in_=KTs[h]), sb=b_KTn[hp], writes=[b_KTn[hp]])
                        S.dma("sp", lambda e, h=h, hp=hp: e.dma_start(
                            out=Vn[hp][:, :, 0:128], in_=PT[h // 4, :, (h % 4) * 128:(h % 4 + 1) * 128].rearrange("(t p) c -> p t c", p=128)),
                            sb=b_Vn[hp], writes=[b_Vn[hp]])
                        S.dma("sp", lambda e, h=h: e.dma_start(out=nabf[:], in_=nab_d[h]), sb=b_nabf, writes=[b_nabf])
                        S.op("act", lambda e, hp=hp: e.activation(out=EB[hp][:], in_=nabf[:], func=AF.Exp), reads=[b_nabf], writes=[b_EB[hp]])
                        for qt in range(8):
                            u = it % 2
                            it += 1
                            if qt == 0:
                                kcs = [0, 1, 2, 3]
                                tb = 6
                            elif qt == 7:
                                kcs = [12, 13, 14, 15]
                                tb = 10
                            else:
                                kcs = list(range(2 * qt - 2, 2 * qt + 4))
                                tb = 0
                            chunks = [(kc + 2, tb + i) for i, kc in enumerate(kcs)] + [(0, None), (1, None)]
                            for ci, (tt, tbi) in enumerate(chunks):
                                shb = ecnt % 6
                                ei = ecnt % 4
                                ecnt += 1
                                sap = ps[shb % 3][:, (shb // 3) * 256:(shb // 3) * 256 + 256]
                                S.op("pe", lambda e, sap=sap, hp=hp, tt=tt, qt=qt: e.matmul(
                                    sap, lhsT=KTn[hp][:, tt * 128:(tt + 1) * 128], rhs=QTn[hp][:, qt * 256:(qt + 1) * 256], start=True, stop=True),
                                    reads=[b_KTn[hp], b_QTn[hp]], writes=[sbk[shb]])
                                if tbi is None:
                                    S.op("act", lambda e, sap=sap, u=u, ci=ci: e.activation(out=Pm[u][:, ci, :], in_=sap, func=AF.Exp, scale=128.0 ** -0.5),
                                         reads=[sbk[shb]], writes=[b_Pm[u][ci]])
                                else:
                                    S.op("act", lambda e, sap=sap, ei=ei: e.activation(out=Ee[ei][:], in_=sap, func=AF.Exp, scale=128.0 ** -0.5),
                                         reads=[sbk[shb]], writes=[b_Ee[ei]])
                                    S.op("dve", lambda e, u=u, ci=ci, ei=ei, hp=hp, tbi=tbi: e.tensor_tensor(
                                        out=Pm[u][:, ci, :], in0=Ee[ei][:], in1=EB[hp][:, tbi, :], op=ALU.mult),
                                        reads=[b_Ee[ei], b_EB[hp]], writes=[b_Pm[u][ci]])
                            nck = len(chunks)
                            for qh in range(2):
                                ob = u * 2 + qh
                                for ci, (tt, tbi) in enumerate(chunks):
                                    S.op("pe", lambda e, ob=ob, u=u, ci=ci, qh=qh, hp=hp, tt=tt, nck=nck: e.matmul(
                                        ps[3 + ob][:, 0:129], lhsT=Pm[u][:, ci, qh * 128:(qh + 1) * 128], rhs=Vn[hp][:, tt, 0:129],
                                        start=(ci == 0), stop=(ci == nck - 1)),
                                        reads=[b_Pm[u][ci], b_Vn[hp], b_Vones[hp]], writes=[obk[ob]], signal=(ci == nck - 1))
                                S.op("dve", lambda e, ob=ob, u=u, qh=qh: e.reciprocal(out=rcp[u][:, qh:qh + 1], in_=ps[3 + ob][:, 128:129]),
                                     reads=[obk[ob]], writes=[b_rcp[u]])
                                S.op("act" if qh == 0 else "dve",
                                     (lambda e, ob=ob, u=u, qh=qh: e.activation(out=on[u][:, qh, :], in_=ps[3 + ob][:, 0:128], func=AF.Copy, scale=rcp[u][:, qh:qh + 1]))
                                     if qh == 0 else
                                     (lambda e, ob=ob, u=u, qh=qh: e.tensor_scalar(out=on[u][:, qh, :], in0=ps[3 + ob][:, 0:128], scalar1=rcp[u][:, qh:qh + 1], scalar2=None, op0=ALU.mult)),
                                     reads=[obk[ob], b_rcp[u]], writes=[b_on[u]])
                            for qh in range(2):
                                S.op("pe", lambda e, u=u, qh=qh: e.transpose(
                                    out=psb(7)[:, (u * 2 + qh) * 128:(u * 2 + qh + 1) * 128], in_=on[u][:, qh, :], identity=ident[:]),
                                    reads=[b_on[u], b_const], writes=[trb], signal=(qh == 1))
                            S.op("dve", lambda e, u=u, h=h, qt=qt: e.tensor_copy(
                                out=mixT[:, h, qt * 256:(qt + 1) * 256], in_=psb(7)[:, u * 256:(u + 1) * 256]),
                                reads=[trb], writes=[b_mix[(h, 2 * qt)], b_mix[(h, 2 * qt + 1)]])
                            cast_step(2)
                    cast_drain(True)
                    if debug and s == 0:
                        bd = S.buf("dbg")
                        S.dma("sp", lambda e: e.dma_start(out=mixdbg, in_=mixT[:]), sb=bd, reads=list(b_mix.values()))
                    S.end_stage()
                    chk("S2b" + ("" if "S2b" == "S0" else str(s)))


                with ExitStack() as st:
                    wo = sb(st, "wo", [128, KC, D], BF16)
                    b_wo = S.bufs(4, "wo")
                    G1 = sb(st, "G1", [128, D], F32)
                    b_G1 = S.buf("G1")
                    xin = [sb(st, f"x3_{i}", [128, D], F32) for i in range(2)]
                    b_x = S.bufs(2, "x3")
                    tt_ = [sb(st, f"t3_{i}", [128, D], F32) for i in range(2)]
                    b_t = S.bufs(2, "t3")
                    xo = [sb(st, f"xo_{i}", [128, D], F32) for i in range(2)]
                    b_xo = S.bufs(2, "xo")
                    junk = sb(st, "junk3", [128, 512], BF16)
                    b_junk = S.buf("junk3")
                    ssp = [sb(st, f"ssp{i}", [128, 8], F32) for i in range(2)]
                    bst3 = [sb(st, f"bst3{i}", [128, 4, 6], F32) for i in range(2)]
                    b_ss = S.bufs(2, "ss3")
                    b_pb = S.bufs(8, "pb3")
                    for q in range(4):
                        S.dma("sp", lambda e, q=q: e.dma_start(
                            out=wo[:, :, q * 512:(q + 1) * 512], in_=woutb[:, q * 512:(q + 1) * 512].rearrange("(j p) c -> p j c", p=128)),
                            sb=b_wo[q], reads=[wbufs["wout"]], writes=[b_wo[q]])
                    S.dma("sp", lambda e, s=s: e.dma_start(out=G1[:], in_=gvec[s, :].partition_broadcast(128)), sb=b_G1, writes=[b_G1])
                    for lt in range(16):
                        u = lt % 2
                        S.dma("sp", lambda e, lt=lt, u=u, s=s: e.dma_start(out=xin[u][:], in_=x2[s, lt * 128:(lt + 1) * 128, :]), sb=b_x[u], writes=[b_x[u]])
                        for q in range(4):
                            bk = u * 4 + q
                            for j in range(KC):
                                S.op("pe", lambda e, bk=bk, j=j, lt=lt, q=q: e.matmul(
                                    ps[bk][:], lhsT=mixT[:, j, lt * 128:(lt + 1) * 128], rhs=wo[:, j, q * 512:(q + 1) * 512],
                                    start=(j == 0), stop=(j == KC - 1)),
                                    reads=[b_mix[(j, lt)], b_wo[q]], writes=[b_pb[bk]], signal=(j == KC - 1))
                            S.op("dve", lambda e, bk=bk, u=u, q=q: e.bn_stats(out=bst3[u][:, q, :], in_=ps[bk][:]),
                                 reads=[b_pb[bk]], writes=[b_ss[u]])
                        S.op("dve", lambda e, u=u: e.bn_aggr(out=ssp[u][:, 0:2], in_=bst3[u][:].rearrange("p q s -> p (q s)")),
                             reads=[b_ss[u]], writes=[b_ss[u]])
                        S.op("dve", lambda e, u=u: e.scalar_tensor_tensor(out=ssp[u][:, 5:6], in0=ssp[u][:, 0:1], scalar=ssp[u][:, 0:1],
                                                                           in1=ssp[u][:, 1:2], op0=ALU.mult, op1=ALU.add),
                             reads=[b_ss[u]], writes=[b_ss[u]])
                        S.op("dve", lambda e, u=u: e.tensor_scalar(out=ssp[u][:, 5:6], in0=ssp[u][:, 5:6], scalar1=EPS, scalar2=None, op0=ALU.add),
                             reads=[b_ss[u]], writes=[b_ss[u]])
                        S.op("act", lambda e, u=u: e.activation(out=ssp[u][:, 5:6], in_=ssp[u][:, 5:6], func=AF.Sqrt),
                             reads=[b_ss[u]], writes=[b_ss[u]])
                        S.op("dve", lambda e, u=u: e.reciprocal(out=ssp[u][:, 6:7], in_=ssp[u][:, 5:6]),
                             reads=[b_ss[u]], writes=[b_ss[u]])
                        for q in range(4):
                            bk = u * 4 + q
                            S.op("dve", lambda e, bk=bk, u=u, q=q: e.scalar_tensor_tensor(
                                out=tt_[u][:, q * 512:(q + 1) * 512], in0=ps[bk][:], scalar=ssp[u][:, 6:7], in1=G1[:, q * 512:(q + 1) * 512],
                                op0=ALU.mult, op1=ALU.mult), reads=[b_pb[bk], b_ss[u], b_G1], writes=[b_t[u]])
                        S.op("dve", lambda e, u=u: e.tensor_tensor(out=xo[u][:], in0=tt_[u][:], in1=xin[u][:], op=ALU.add),
                             reads=[b_t[u], b_x[u]], writes=[b_xo[u]])
                        S.dma("sp", lambda e, lt=lt, u=u, s=s: e.dma_start(out=xnew[s, lt * 128:(lt + 1) * 128, :], in_=xo[u][:]), sb=b_xo[u], reads=[b_xo[u]])
                    S.end_stage()
                    chk("S3" + ("" if "S3" == "S0" else str(s)))

        with ExitStack() as st:
            h2T = sb(st, "h2T", [128, KC, 512], BF16)
            actT = sb(st, "actT", [128, FC, 512], BF16)
            ring = [sb(st, f"ring{i}", [128, 22 * 512], BF16) for i in range(4)]
            b_ring = S.bufs(4, "ring")
            G2 = sb(st, "G2", [128, D], F32)
            b_G2 = S.buf("G2")
            ybuf = [sb(st, f"yb{i}", [128, D], F32) for i in range(2)]
            b_yb = S.bufs(2, "yb")
            xn = [sb(st, f"xn{i}", [128, D], F32) for i in range(2)]
            b_xn = S.bufs(2, "xn")
            ssq = [sb(st, f"ssq{i}", [128, 8], F32) for i in range(2)]
            bst4 = [sb(st, f"bst4{i}", [128, 4, 6], F32) for i in range(2)]
            b_sq = S.bufs(2, "sq")
            sg = [sb(st, f"sg{i}", [128, 512], F32) for i in range(2)]
            b_sg = S.bufs(2, "sg")
            junk = sb(st, "junk4", [128, 512], BF16)
            b_junk = S.buf("junk4")
            b_act = S.bufs(FC, "act")
            b_pb = S.bufs(8, "pb4")
            rc = 0
            xcnt = 0
            hb2 = {(j, ti): S.buf("h2") for j in range(KC) for ti in range(4)}
            ntile4 = make_norm(st, "f", h2T, [0, 1, 2, 3], b_pb[0:4], xin_ext=(xn, b_xn))
            def _nsrc(ftn, ti):
                sn, tn = ftn // 4, (ftn % 4) * 512
                return xnew[sn, tn + ti * 128:tn + (ti + 1) * 128, :]

            def norm_early(ftn):
                ntile4.load(_nsrc(ftn, 0))
                ntile4.load(_nsrc(ftn, 1))
                ntile4.stats()
                ntile4.stats()

            def norm_late(ftn):
                sn = ftn // 4
                hbj = lambda ti: [hb2[(j, ti)] for j in range(KC)]
                ntile4.xpose(6 + sn, 8 + sn, 0, hbj(0))
                ntile4.load(_nsrc(ftn, 2))
                ntile4.stats()
                ntile4.xpose(6 + sn, 8 + sn, 128, hbj(1))
                ntile4.load(_nsrc(ftn, 3))
                ntile4.stats()
                ntile4.xpose(6 + sn, 8 + sn, 256, hbj(2))
                ntile4.xpose(6 + sn, 8 + sn, 384, hbj(3))

            for ft in range(8):
                s = ft // 4
                t0 = (ft % 4) * 512
                if ft % 4 == 0:
                    S.dma("sp", lambda e, s=s: e.dma_start(out=G2[:], in_=gvec[2 + s, :].partition_broadcast(128)), sb=b_G2, writes=[b_G2])
                if ft == 0:
                    norm_early(0)
                    norm_late(0)
                h2all = list(hb2.values())
                for cgp in range(FC // 4):
                    rg, ru = rc % 4, (rc + 1) % 4
                    rc += 2
                    for r, wsrc, nm in ((rg, wgb, "wg"), (ru, wub, "wu")):
                        S.dma("sp", lambda e, r=r, wsrc=wsrc, cgp=cgp: e.dma_start(
                            out=ring[r][:, 0:KC * 512].rearrange("p (j c) -> p j c", c=512),
                            in_=wsrc[:, cgp * 512:(cgp + 1) * 512].rearrange("(j p) c -> p j c", p=128)),
                            sb=b_ring[r], reads=[wbufs[nm]], writes=[b_ring[r]])
                    for cc in range(4):
                        c = cgp * 4 + cc
                        u = c % 2
                        bg, bu = u * 2, u * 2 + 1
                        for r, bk in ((rg, bg), (ru, bu)):
                            wv = ring[r][:, 0:KC * 512].rearrange("p (j c) -> p j c", c=512)
                            for j in range(KC):
                                S.op("pe", lambda e, wv=wv, bk=bk, j=j, cc=cc: e.matmul(
                                    ps[bk][:], lhsT=wv[:, j, cc * 128:(cc + 1) * 128], rhs=h2T[:, j, :], start=(j == 0), stop=(j == KC - 1)),
                                    reads=[b_ring[r]] + ([] if j else h2all), writes=[b_pb[bk]], signal=(j == KC - 1))
                        S.op("act", lambda e, u=u, bg=bg: e.activation(out=sg[u][:], in_=ps[bg][:], func=AF.Silu),
                             reads=[b_pb[bg]], writes=[b_sg[u]])
                        S.op("dve", lambda e, u=u, bu=bu, c=c: e.tensor_tensor(out=actT[:, c, :], in0=ps[bu][:], in1=sg[u][:], op=ALU.mult),
                             reads=[b_pb[bu], b_sg[u]], writes=[b_act[c]])
                for tp in range(2):
                    if ft + 1 < 8:
                        if tp == 0:
                            norm_early(ft + 1)
                        else:
                            norm_late(ft + 1)
                    for dg in range(4):
                        for half in range(2):
                            r = rc % 4
                            rc += 1
                            S.dma("sp", lambda e, r=r, dg=dg, half=half: e.dma_start(
                                out=ring[r][:].rearrange("p (c n) -> p c n", n=512),
                                in_=wdb[half * 22 * 128:(half + 1) * 22 * 128, dg * 512:(dg + 1) * 512].rearrange("(c p) n -> p c n", p=128)),
                                sb=b_ring[r], reads=[wbufs["wd"]], writes=[b_ring[r]])
                            wv = ring[r][:].rearrange("p (c n) -> p c n", n=512)
                            for tl in range(2):
                                ti = tp * 2 + tl
                                bk = 4 + (dg % 2) * 2 + tl
                                for c22 in range(22):
                                    c = half * 22 + c22
                                    S.op("pe", lambda e, wv=wv, bk=bk, c=c, c22=c22, ti=ti: e.matmul(
                                        ps[bk][:], lhsT=actT[:, c, ti * 128:(ti + 1) * 128], rhs=wv[:, c22, :],
                                        start=(c == 0), stop=(c == FC - 1)),
                                        reads=[b_ring[r], b_act[c]], writes=[b_pb[bk]], signal=(c22 == 21))
                        for tl in range(2):
                            bk = 4 + (dg % 2) * 2 + tl
                            S.op("dve", lambda e, bk=bk, tl=tl, dg=dg: e.bn_stats(out=bst4[tl][:, dg, :], in_=ps[bk][:]),
                                 reads=[b_pb[bk]], writes=[b_sq[tl]])
                            S.op("dve", lambda e, bk=bk, tl=tl, dg=dg: e.tensor_tensor(
                                out=ybuf[tl][:, dg * 512:(dg + 1) * 512], in0=ps[bk][:], in1=G2[:, dg * 512:(dg + 1) * 512], op=ALU.mult),
                                reads=[b_pb[bk], b_G2], writes=[b_yb[tl]])
                    for tl in range(2):
                        ti = tp * 2 + tl
                        xi = xcnt % 2
                        xcnt += 1
                        S.dma("sp", lambda e, xi=xi, s=s, t0=t0, ti=ti: e.dma_start(out=xn[xi][:], in_=xnew[s, t0 + ti * 128:t0 + (ti + 1) * 128, :]),
                              sb=b_xn[xi], writes=[b_xn[xi]])
                        S.op("dve", lambda e, tl=tl: e.bn_aggr(out=ssq[tl][:, 0:2], in_=bst4[tl][:].rearrange("p q s -> p (q s)")),
                             reads=[b_sq[tl]], writes=[b_sq[tl]])
                        S.op("dve", lambda e, tl=tl: e.scalar_tensor_tensor(out=ssq[tl][:, 5:6], in0=ssq[tl][:, 0:1], scalar=ssq[tl][:, 0:1],
                                                                             in1=ssq[tl][:, 1:2], op0=ALU.mult, op1=ALU.add),
                             reads=[b_sq[tl]], writes=[b_sq[tl]])
                        S.op("dve", lambda e, tl=tl: e.tensor_scalar(out=ssq[tl][:, 5:6], in0=ssq[tl][:, 5:6], scalar1=EPS, scalar2=None, op0=ALU.add),
                             reads=[b_sq[tl]], writes=[b_sq[tl]])
                        S.op("act", lambda e, tl=tl: e.activation(out=ssq[tl][:, 5:6], in_=ssq[tl][:, 5:6], func=AF.Sqrt),
                             reads=[b_sq[tl]], writes=[b_sq[tl]])
                        S.op("dve", lambda e, tl=tl: e.reciprocal(out=ssq[tl][:, 6:7], in_=ssq[tl][:, 5:6]),
                             reads=[b_sq[tl]], writes=[b_sq[tl]])
                        S.op("dve", lambda e, tl=tl, xi=xi: e.scalar_tensor_tensor(
                            out=xn[xi][:], in0=ybuf[tl][:], scalar=ssq[tl][:, 6:7], in1=xn[xi][:], op0=ALU.mult, op1=ALU.add),
                            reads=[b_yb[tl], b_sq[tl], b_xn[xi]], writes=[b_xn[xi]])
                        S.dma("sp", lambda e, xi=xi, s=s, t0=t0, ti=ti: e.dma_start(out=out2[s, t0 + ti * 128:t0 + (ti + 1) * 128, :], in_=xn[xi][:]),
                              sb=b_xn[xi], reads=[b_xn[xi]])
            S.barrier(final=True)

        with nc.Block() as block:
            S.emit(block)
    return nc


def _consts():
    p = np.arange(128)
    i = p[None, :].astype(np.float32)
    j = p[:, None].astype(np.float32)
    rcon = np.zeros((128, 5, 128), np.float32)
    rcon[:, 0] = np.maximum(i - j, 0)
    rcon[:, 1] = np.maximum(j - i, 0)
    rcon[:, 2] = (i > j) * 0.125
    rcon[:, 3] = (j > i) * 0.125
    rcon[:, 4] = (i == j) * 0.25
    pidx = np.stack([127 - p, p, p + 1, 128 - p], axis=1).astype(np.float32)
    quarter = 16
    inv = (10000.0 ** (-np.arange(quarter, dtype=np.float32) / quarter)).astype(np.float32)
    t = (np.arange(16)[None, :] * 128 + p[:, None])
    row = (t // 64).astype(np.float32)[..., None] * inv
    col = (t % 64).astype(np.float32)[..., None] * inv
    cr, sr, cc, sc = np.cos(row), np.sin(row), np.cos(col), np.sin(col)
    ropeC = np.concatenate([cr, cr, cc, cc], axis=-1).astype(np.float32)
    ropeS = np.concatenate([-sr, sr, -sc, sc], axis=-1).astype(np.float32)
    ident = np.eye(128, dtype=np.float32)
    return rcon, pidx, ropeC, ropeS, ident


def _nab_index():
    key = np.arange(128)[:, None, None]
    tb = np.arange(14)[None, :, None]
    q = np.arange(256)[None, None, :]
    kr, kc = key // 64, key % 64
    qr, qc = q // 64, q % 64
    dr_int = -4 + 2 * tb + kr - qr
    dr_top = 2 * (tb - 6) + kr - qr
    dr_bot = -4 + 2 * (tb - 10) + kr - qr
    dr = np.where(tb < 6, dr_int, np.where(tb < 10, dr_top, dr_bot))
    vr = np.where(tb < 6, (dr >= -4) & (dr <= 3), True)
    cs = np.clip(qc - 8, 0, 48)
    vc = (kc >= cs) & (kc < cs + 16)
    dc = kc - qc
    valid = vr & vc & (np.abs(dr) <= 7)
    idx_r = np.clip(dr + 7, 0, 14)
    idx_c = np.clip(dc + 15, 0, 30)
    return np.broadcast_to(idx_r, (128, 14, 256)), np.broadcast_to(idx_c, (128, 14, 256)), np.broadcast_to(valid, (128, 14, 256))


_NC_CACHE = {}


def _prep_inputs(inputs):
    f = lambda a: np.ascontiguousarray(np.asarray(a, dtype=np.float32))
    x, c, ctx, c_ctx = f(inputs["x"]), f(inputs["c"]), f(inputs["ctx"]), f(inputs["c_ctx"])
    rcon, pidx, ropeC, ropeS, ident = _consts()
    ada_w = f(inputs["ada_w"][0])
    ada_bT = f(inputs["ada_b"][0].reshape(96, 128).T)
    g4 = np.stack([inputs["norm_pre_mix"][0], inputs["norm_post_mix"][0], inputs["norm_pre_ffn"][0], inputs["norm_post_ffn"][0]])
    gT = f(np.asarray(g4, np.float32).reshape(4, 16, 128).transpose(2, 0, 1))
    lg = np.concatenate([np.asarray(inputs["ret_log_gamma_fwd"][0]), np.asarray(inputs["ret_log_gamma_bwd"][0])]).astype(np.float32)
    lgr = f(np.broadcast_to(lg[None, :], (128, 16)))
    rpb = np.asarray(inputs["na_rpb"][0], np.float32)
    ir, ic, valid = _nab_index()
    nab = np.where(valid[None], rpb[:, ir, ic], np.float32(NEG)).astype(np.float32)
    shared = {
        "ada_w": ada_w, "ada_bT": ada_bT, "gT": gT,
        "ada_b3": f(np.broadcast_to(np.asarray(inputs["ada_b"][0], np.float32)[None, :], (3, 6 * D))),
        "w_in": f(inputs["w_in"][0]), "w_out": f(inputs["w_out"][0]), "w_gate": f(inputs["w_gate"][0]),
        "w_up": f(inputs["w_up"][0]), "w_down": f(inputs["w_down"][0]),
        "nab": f(nab), "lgr": lgr, "ropeC": ropeC, "ropeS": ropeS, "ident": ident, "rcon": rcon, "pidx": pidx,
    }
    in_maps = []
    for i in range(8):
        cc3 = np.stack([c[2 * i], c[2 * i + 1], c_ctx], axis=0)
        cT = f(cc3.reshape(3, 16, 128).transpose(2, 1, 0))
        m = dict(shared)
        m["x2"] = f(x[2 * i:2 * i + 2])
        m["ctx2"] = f(ctx[2 * i:2 * i + 2])
        m["cT"] = cT
        in_maps.append(m)
    return in_maps


def kernel(**inputs):
    in_maps = _prep_inputs(inputs)
    if "nc" not in _NC_CACHE:
        _NC_CACHE["nc"] = build()
    nc = _NC_CACHE["nc"]
    res = run_bass_kernel_spmd(nc, in_maps, core_ids=list(range(8)))
    out = np.concatenate([np.asarray(r["out2"], dtype=np.float32) for r in res.results], axis=0)
    return out
```
